# Optimizing a Trainium2 kernel written in Bass

```python
import math
import jax, jax.numpy as jnp
from jax import lax
import numpy as np

D_MODEL = 2048
BATCH = 2
SEQ = 16384
DEPTH = 1

CONV_WIDTH = 1024
CONV_TAPS = 3
ATT_HEADS = 4
ATT_HEAD_DIM = 128
ATT_WIDTH = ATT_HEADS * 2 * ATT_HEAD_DIM
IN_WIDTH = 3 * CONV_WIDTH + 3 * ATT_WIDTH
IN_SPLITS = [CONV_WIDTH, 2 * CONV_WIDTH, 3 * CONV_WIDTH,
             3 * CONV_WIDTH + ATT_WIDTH, 3 * CONV_WIDTH + 2 * ATT_WIDTH]
N_BRANCHES = 2
D_FF = 5632
PLE_DIM = 256
N_BUCKETS = 32
MAX_DISTANCE = 128
Q_BLOCK = 128
EPS = 1e-6

kernel_name = 'hybrid_conv_diffattn_macaron_encoder'


def rms_norm(x, g):
    xf = x.astype(jnp.float32)
    y = xf * lax.rsqrt(jnp.mean(xf * xf, axis=-1, keepdims=True) + EPS)
    return (y * g.astype(jnp.float32)).astype(x.dtype)


def swiglu(x, w1, w3, w2):
    return (jax.nn.silu(x @ w1) * (x @ w3)) @ w2


def short_conv(z, w):
    zp = jnp.pad(z, ((0, 0), (1, 1), (0, 0)))
    return zp[:, :-2] * w[0] + zp[:, 1:-1] * w[1] + zp[:, 2:] * w[2]


def t5_bucket(rel):
    nb = N_BUCKETS // 2
    max_exact = nb // 2
    ret = jnp.where(rel > 0, nb, 0).astype(jnp.int32)
    n = jnp.abs(rel)
    nf = jnp.maximum(n, 1).astype(jnp.float32)
    large = max_exact + (jnp.log(nf / max_exact) / math.log(MAX_DISTANCE / max_exact)
                         * (nb - max_exact)).astype(jnp.int32)
    large = jnp.minimum(large, nb - 1)
    return ret + jnp.where(n < max_exact, n, large)


def diff_attention(q, k, v, lam, rel_bias):
    B, S = q.shape[0], q.shape[1]
    nblk = S // Q_BLOCK
    q_blocks = q.reshape(B, nblk, Q_BLOCK, ATT_HEADS, 2, ATT_HEAD_DIM).swapaxes(0, 1)
    k_pos = jnp.arange(S, dtype=jnp.int32)
    bias_table = rel_bias.T.astype(jnp.float32)
    scale = ATT_HEAD_DIM ** -0.5

    def block(args):
        q_blk, i = args
        q_pos = i * Q_BLOCK + jnp.arange(Q_BLOCK, dtype=jnp.int32)
        bias = bias_table[:, t5_bucket(k_pos[None, :] - q_pos[:, None])]
        logits = jnp.einsum('bqhcd,bkhcd->bchqk', q_blk, k).astype(jnp.float32) * scale + bias
        probs = jax.nn.softmax(logits, axis=-1)
        w = probs[:, 0] - lam.astype(jnp.float32) * probs[:, 1]
        return jnp.einsum('bhqk,bkhe->bqhe', w.astype(v.dtype), v)

    out = lax.map(block, (q_blocks, jnp.arange(nblk, dtype=jnp.int32)))
    return out.swapaxes(0, 1).reshape(B, S, ATT_HEADS, 2 * ATT_HEAD_DIM)


def setup_inputs(seed: int = 0) -> dict:
    key = jax.random.key(seed)
    ks = iter(jax.random.split(key, 32))

    def lin(fan_in, fan_out):
        return jax.random.normal(next(ks), (DEPTH, fan_in, fan_out), jnp.float32) * fan_in ** -0.5

    def gain(dim):
        return 1.0 + 0.02 * jax.random.normal(next(ks), (DEPTH, dim), jnp.float32)

    def small(shape, s):
        return s * jax.random.normal(next(ks), shape, jnp.float32)

    return {
        'x': jax.random.normal(next(ks), (BATCH, SEQ, D_MODEL), jnp.float32),
        'p': jax.random.normal(next(ks), (DEPTH, BATCH, SEQ, PLE_DIM), jnp.float32),
        'ffn1_norm': gain(D_MODEL),
        'ffn1_w1': lin(D_MODEL, D_FF),
        'ffn1_w3': lin(D_MODEL, D_FF),
        'ffn1_w2': lin(D_FF, D_MODEL),
        'mix_norm': gain(D_MODEL),
        'w_in': lin(D_MODEL, IN_WIDTH),
        'conv_w': small((DEPTH, CONV_TAPS, CONV_WIDTH), CONV_TAPS ** -0.5),
        'q_norm': gain(ATT_HEAD_DIM),
        'k_norm': gain(ATT_HEAD_DIM),
        'lam_q1': small((DEPTH, ATT_HEAD_DIM), 0.1),
        'lam_k1': small((DEPTH, ATT_HEAD_DIM), 0.1),
        'lam_q2': small((DEPTH, ATT_HEAD_DIM), 0.1),
        'lam_k2': small((DEPTH, ATT_HEAD_DIM), 0.1),
        'sub_norm': gain(2 * ATT_HEAD_DIM),
        'rel_bias': small((N_BUCKETS, ATT_HEADS), 0.5),
        'w_branch_a': lin(CONV_WIDTH, D_MODEL),
        'w_branch_b': lin(ATT_WIDTH, D_MODEL),
        'w_gate': lin(D_MODEL, N_BRANCHES * D_MODEL),
        'w_out': lin(D_MODEL, D_MODEL),
        'ffn2_norm': gain(D_MODEL),
        'ffn2_w1': lin(D_MODEL, D_FF),
        'ffn2_w3': lin(D_MODEL, D_FF),
        'ffn2_w2': lin(D_FF, D_MODEL),
        'ple_norm': gain(D_MODEL),
        'w_ple_gate': lin(D_MODEL, D_MODEL),
        'w_ple_proj': lin(PLE_DIM, D_MODEL),
    }


def reference(x, p, ffn1_norm, ffn1_w1, ffn1_w3, ffn1_w2, mix_norm, w_in, conv_w,
              q_norm, k_norm, lam_q1, lam_k1, lam_q2, lam_k2, sub_norm, rel_bias,
              w_branch_a, w_branch_b, w_gate, w_out, ffn2_norm, ffn2_w1, ffn2_w3, ffn2_w2,
              ple_norm, w_ple_gate, w_ple_proj):
    B, S = x.shape[0], x.shape[1]
    h = x
    for l in range(DEPTH):
        h = h + 0.5 * swiglu(rms_norm(h, ffn1_norm[l]), ffn1_w1[l], ffn1_w3[l], ffn1_w2[l])

        u = rms_norm(h, mix_norm[l])
        a_in, c_gate, b_gate, q, k, v = jnp.split(u @ w_in[l], IN_SPLITS, axis=-1)

        y_a = (b_gate * short_conv(c_gate * a_in, conv_w[l])) @ w_branch_a[l]

        q = rms_norm(q.reshape(B, S, ATT_HEADS, 2, ATT_HEAD_DIM), q_norm[l])
        k = rms_norm(k.reshape(B, S, ATT_HEADS, 2, ATT_HEAD_DIM), k_norm[l])
        v = v.reshape(B, S, ATT_HEADS, 2 * ATT_HEAD_DIM)
        lam_init = 0.8 - 0.6 * math.exp(-0.3 * l)
        lam = (jnp.exp(jnp.sum(lam_q1[l] * lam_k1[l])) - jnp.exp(jnp.sum(lam_q2[l] * lam_k2[l]))
               + lam_init)
        o = diff_attention(q, k, v, lam, rel_bias)
        o = rms_norm(o, sub_norm[l]) * (1.0 - lam_init)
        y_b = o.reshape(B, S, ATT_WIDTH) @ w_branch_b[l]

        g_a, g_b = jnp.split(jax.nn.sigmoid(u @ w_gate[l]), N_BRANCHES, axis=-1)
        h = h + (g_a * y_a + g_b * y_b) @ w_out[l]

        h = h + 0.5 * swiglu(rms_norm(h, ffn2_norm[l]), ffn2_w1[l], ffn2_w3[l], ffn2_w2[l])

        gate = jax.nn.sigmoid(rms_norm(h, ple_norm[l]) @ w_ple_gate[l])
        h = h + gate * (p[l] @ w_ple_proj[l])
    return h
```

```python
import math
from contextlib import ExitStack

import numpy as np
import concourse.bass as bass
import concourse.mybir as mybir
from concourse.bass_utils import run_bass_kernel_spmd

F32 = mybir.dt.float32
BF16 = mybir.dt.bfloat16
I32 = mybir.dt.int32
AF = mybir.ActivationFunctionType
ALU = mybir.AluOpType

P = 128
D = 2048
KD = 16
FF = 5632
KF = 44
T = 512
TOK = 4096
NT = TOK // T
CW = 1024
AW = 1024
NH = 4
PLE = 256
EPS = 1e-6
LAM_INIT = 0.8 - 0.6 * math.exp(-0.3 * 0)
NCLS = 6
GVW = 640

ENGS = ("pe", "act", "dve", "pool", "sp")


class Buf:
    __slots__ = ("name", "writer", "readers", "sem")

    def __init__(self, name):
        self.name = name
        self.writer = None
        self.readers = {}
        self.sem = None


class SemSlot:
    __slots__ = ("handle", "count", "name")

    def __init__(self, name):
        self.name = name
        self.handle = None
        self.count = 0


class Ins:
    __slots__ = ("fn", "deps", "signaled", "cum", "dma_sem", "inc")

    def __init__(self, fn, deps):
        self.fn = fn
        self.deps = deps
        self.signaled = False
        self.cum = 0
        self.dma_sem = None
        self.inc = 16


class NullSched:
    null = True

    def op(self, *a, **k):
        return None

    def dma(self, *a, **k):
        return None

    def coll(self, *a, **k):
        return None

    def fence(self, *a, **k):
        return None


class Sched:
    null = False

    def __init__(self, nc):
        self.nc = nc
        self.streams = {e: [] for e in ENGS}
        self.sems = []
        self.eng_sem = {e: SemSlot("eng_" + e) for e in ENGS}

    def _deps(self, eng, reads, writes, extra):
        deps = set()
        for b in reads:
            if b.writer is not None:
                deps.add(b.writer)
        for b in writes:
            if b.writer is not None:
                deps.add(b.writer)
            deps.update(b.readers.values())
        for e in extra:
            if e is not None:
                deps.add(e)
        if eng == "pe":
            deps = {d for d in deps if not (d[0] == "E" and d[1] == "pe")}
        return deps

    def op(self, eng, fn, reads=(), writes=(), extra=()):
        deps = self._deps(eng, reads, writes, extra)
        st = self.streams[eng]
        ev = ("E", eng, len(st))
        st.append(Ins(fn, deps))
        for b in reads:
            b.readers[eng] = ev
        for b in writes:
            b.writer = ev
            b.readers = {}
        return ev

    def _slot(self, sem_buf):
        if sem_buf.sem is None:
            sem_buf.sem = SemSlot("d_" + sem_buf.name)
            self.sems.append(sem_buf.sem)
        return sem_buf.sem

    def dma(self, queue, fn, sem_buf, reads=(), writes=(), extra=(), batch=False):
        deps = self._deps("dma", reads, writes, extra)
        slot = self._slot(sem_buf)
        if batch:
            deps = {d for d in deps if not (d[0] == "D" and d[1] is slot)}
        slot.count += 16
        ins = Ins(fn, deps)
        ins.dma_sem = slot
        self.streams[queue].append(ins)
        ev = ("D", slot, slot.count)
        for b in reads:
            b.readers[("dma", id(slot))] = ev
        for b in writes:
            b.writer = ev
            b.readers = {}
        return ev

    def coll(self, fn, sem_buf, reads=(), writes=()):
        deps = self._deps("dma", reads, writes, ())
        slot = self._slot(sem_buf)
        slot.count += 1
        ins = Ins(fn, deps)
        ins.dma_sem = slot
        ins.inc = 1
        self.streams["pool"].append(ins)
        ev = ("D", slot, slot.count)
        for b in reads:
            b.readers[("dma", id(slot))] = ev
        for b in writes:
            b.writer = ev
            b.readers = {}
        return ev

    def fence(self, old_bufs, new_bufs):
        evs = []
        for b in old_bufs:
            if b.writer is not None:
                evs.append(b.writer)
            evs.extend(b.readers.values())
        evs = list(set(evs))
        self.nfence = getattr(self, "nfence", 0) + 1
        for b in new_bufs:
            for i, ev in enumerate(evs):
                b.readers[("fence", self.nfence, i)] = ev

    def emit(self, stack):
        nc = self.nc
        for e in ENGS:
            for ins in self.streams[e]:
                for d in ins.deps:
                    if d[0] == "E":
                        self.streams[d[1]][d[2]].signaled = True
        for e in ENGS:
            c = 0
            for ins in self.streams[e]:
                if ins.signaled:
                    c += 1
                ins.cum = c
        for e in ENGS:
            self.eng_sem[e].handle = stack.enter_context(nc.semaphore("sem_" + e))
        for s in self.sems:
            s.handle = stack.enter_context(nc.semaphore(s.name))
        block = stack.enter_context(nc.Block())
        stats = {}

        def run(engname, eng):
            waited = {}
            nw = 0
            for ins in self.streams[engname]:
                need = {}
                for d in ins.deps:
                    if d[0] == "E":
                        slot = self.eng_sem[d[1]]
                        cnt = self.streams[d[1]][d[2]].cum
                    else:
                        slot = d[1]
                        cnt = d[2]
                    if need.get(slot, 0) < cnt:
                        need[slot] = cnt
                for slot, cnt in need.items():
                    if waited.get(slot, 0) >= cnt:
                        continue
                    eng.wait_ge(slot.handle, cnt)
                    waited[slot] = cnt
                    nw += 1
                if ins.fn is None:
                    continue
                bi = ins.fn(eng)
                if bi is None:
                    assert not ins.signaled and ins.dma_sem is None
                    continue
                if ins.dma_sem is not None:
                    if ins.inc == 1:
                        bi.then_inc(ins.dma_sem.handle)
                    else:
                        bi.then_inc(ins.dma_sem.handle, 16)
                elif ins.signaled:
                    bi.then_inc(self.eng_sem[engname].handle, 1)
            stats[engname] = (len(self.streams[engname]), nw)

        @block.tensor
        def _(pe):
            run("pe", pe)

        @block.scalar
        def _(act):
            run("act", act)

        @block.vector
        def _(dve):
            run("dve", dve)

        @block.gpsimd
        def _(pool):
            run("pool", pool)

        @block.sync
        def _(sp):
            run("sp", sp)

        return stats


def _t5_bucket_np(rel):
    nb = 16
    max_exact = 8
    rel = np.asarray(rel, dtype=np.int64)
    ret = np.where(rel > 0, nb, 0).astype(np.int64)
    n = np.abs(rel)
    nf = np.maximum(n, 1).astype(np.float32)
    val = (np.log(nf / np.float32(max_exact)) / np.float32(math.log(128 / max_exact))
           * np.float32(nb - max_exact)).astype(np.float32)
    large = max_exact + val.astype(np.int32).astype(np.int64)
    large = np.minimum(large, nb - 1)
    return ret + np.where(n < max_exact, n, large)


def _onehot_const():
    oh = np.zeros((32, NCLS * GVW), dtype=np.float32)
    for cls in range(NCLS):
        delta = (cls - 1) * 128
        m = np.arange(GVW)
        bk = _t5_bucket_np(delta + m - 511)
        oh[bk, cls * GVW + m] = 1.0
    return oh


class Ring:
    def __init__(self, items):
        self.items = items
        self.i = 0

    def next(self):
        it = self.items[self.i % len(self.items)]
        self.i += 1
        return it


class WStream:
    def __init__(self, S, slots, plan, resolve):
        self.S = S
        self.slots = slots
        self.R = len(slots)
        self.record = plan is None
        self.plan = [] if plan is None else plan
        self.resolve = resolve
        self.next_load = 0
        self.next_use = 0

    def _load(self, m):
        tag, src, KC, NC, key = self.plan[m]
        ap, buf = self.slots[m % self.R]
        dst = ap[:, 0:KC * NC].rearrange("p (k n) -> p k n", n=NC)
        self.S.dma("sp", lambda e, dst=dst, src=src: e.dma_start(out=dst, in_=src), buf, writes=[buf], extra=[self.resolve(key)])

    def get(self, tag, src, KC, NC, key):
        n = self.next_use
        self.next_use += 1
        ap, buf = self.slots[n % self.R]
        view = ap[:, 0:KC * NC].rearrange("p (k n) -> p k n", n=NC)
        if self.record:
            self.plan.append((tag, src, KC, NC, key))
            return view, buf
        assert self.plan[n][0] == tag, (self.plan[n][0], tag)
        lim = min(len(self.plan), n + self.R - 1)
        while self.next_load < lim:
            self._load(self.next_load)
            self.next_load += 1
        return view, buf


def build_program(debug=False, nt1=NT, nt2=NT):
    nc = bass.Bass("TRN2", target_bir_lowering=False)

    def din(name, shape, dt=F32):
        return nc.dram_tensor(name, list(shape), dt, kind="ExternalInput").ap()

    def dscr(name, shape, dt, dbg=False):
        if dbg and debug and (not _DBGSET or name in _DBGSET):
            return nc.dram_tensor(name, list(shape), dt, kind="ExternalOutput").ap()
        return nc.dram_tensor(name, list(shape), dt).ap()

    xT = din("xT", [D, TOK])
    pT = din("pT", [PLE, TOK])
    wsrc = {
        "w1a": din("ffn1_w1", [D, FF]), "w3a": din("ffn1_w3", [D, FF]), "w2a": din("ffn1_w2", [FF, D]),
        "win": din("w_in", [D, 6144]), "wg": din("w_gate", [D, 2 * D]),
        "wa": din("w_branch_a", [CW, D]), "wb": din("w_branch_b", [AW, D]), "wo": din("w_out", [D, D]),
        "w1b": din("ffn2_w1", [D, FF]), "w3b": din("ffn2_w3", [D, FF]), "w2b": din("ffn2_w2", [FF, D]),
        "wpg": din("w_ple_gate", [D, D]), "wpp": din("w_ple_proj", [PLE, D]),
    }
    gains_d = din("gains", [P, 64])
    small_d = din("small", [P, 40])
    relb_d = din("relb", [P, 128])
    tab_d = din("tab32", [32, 4])
    oh_d = din("onehot", [32, NCLS * GVW])
    sel_d = din("sel", [P, 8])
    offs_d = din("offs", [1, 16], I32)
    outT = nc.dram_tensor("outT", [D, TOK], F32, kind="ExternalOutput").ap()

    wbf = {k: dscr("b16_" + k, v.shape, BF16) for k, v in wsrc.items()}
    Hs = dscr("Hs", [D, TOK], F32, dbg=True)
    Zs = dscr("Zs", [CW, TOK], BF16, dbg=True)
    Bs = dscr("Bs", [CW, TOK], BF16, dbg=True)
    QTs = dscr("QTs", [AW, TOK], BF16, dbg=True)
    GAs = dscr("GAs", [D, TOK], BF16, dbg=True)
    GBs = dscr("GBs", [D, TOK], BF16, dbg=True)
    KTl = dscr("KTl", [NT, AW, T], BF16)
    Vl = dscr("Vl", [NT, T, AW], BF16)
    KTall = dscr("KTall", [NT, 4, AW, T], BF16)
    Vall = dscr("Vall", [NT, 4, T, AW], BF16)
    KTrot = dscr("KTrot", [3, NT, AW, T], BF16)
    Vrot = dscr("Vrot", [3, NT, T, AW], BF16)
    zhalo = dscr("zhalo", [2, CW], BF16)
    zedge = dscr("zedge", [2, CW], BF16)
    zedge_all = dscr("zedge_all", [8, CW], BF16)
    GV = dscr("GV", [4, NCLS * GVW], F32)
    BT = dscr("BT", [32, P, T], F32, dbg=True)
    KTd = dscr("KTd", [NT, AW, T], BF16, dbg=True) if debug else None
    Vd = dscr("Vd", [NT, T, AW], BF16, dbg=True) if debug else None
    ONd = dscr("ONd", [AW, TOK], BF16, dbg=True) if debug else None
    MIXd = dscr("MIXd", [D, TOK], BF16, dbg=True) if debug else None
    H2d = dscr("H2d", [D, TOK], F32, dbg=True) if debug else None

    st = ExitStack()
    with st:
        def sb(name, shape, dt):
            return st.enter_context(nc.sbuf_tensor(name, list(shape), dt))

        xh = sb("xh", [P, KD, T], F32)
        xn = sb("xn", [P, KD, T], BF16)
        R1 = sb("R1", [P, KF * T], BF16)
        wr_t = [sb(f"wr{i}", [P, 4096], BF16) for i in range(5)]
        stg_t = [sb(f"stg{i}", [P, 4, T], BF16) for i in range(3)]
        tmp_t = [sb(f"tmp{i}", [P, T], F32) for i in range(5)]
        acc = sb("acc", [P, T], F32)
        rstd = sb("rstd", [P, T], F32)
        RZ = sb("RZ", [P, 12416], BF16)
        VO = sb("VO", [P, 4 * 1024], BF16)
        btl_t = [sb(f"btl{i}", [P, T], F32) for i in range(2)]
        pTb = sb("pTb", [P, 2, T], BF16)
        of_t = [sb(f"of{i}", [P, T], F32) for i in range(2)]
        og_t = [sb(f"og{i}", [P, T], F32) for i in range(2)]
        gains = sb("gains_s", [P, 64], F32)
        small = sb("small_s", [P, 40], F32)
        relb = sb("relb_s", [P, 128], F32)
        sel = sb("sel_s", [P, 8], F32)
        ones = sb("ones_s", [P, P], F32)
        cst = sb("cst_s", [P, 32], F32)
        tab32 = sb("tab32_s", [32, 4], F32)
        bank_t = [st.enter_context(nc.psum_tensor(f"bank{i}", [P, T], F32)) for i in range(8)]

        G = R1[:, :].rearrange("p (j t) -> p j t", t=T)
        qsb = R1[:, 0:4096].rearrange("p (c t) -> p c t", t=T)
        kvK = [R1[:, 4096 + i * 4096: 4096 + i * 4096 + 2048].rearrange("p (c t) -> p c t", t=1024) for i in range(3)]
        kvV = [R1[:, 4096 + i * 4096 + 2048: 4096 + (i + 1) * 4096].rearrange("p (b e) -> p b e", e=256) for i in range(3)]
        E_ap = [R1[:, 16384 + i * T: 16384 + (i + 1) * T] for i in range(4)]
        acs_ap = [R1[:, 18432 + i * 1024: 18432 + (i + 1) * 1024].bitcast(F32) for i in range(4)]
        ga_ap = [R1[:, i * 2048:(i + 1) * 2048].rearrange("p (c t) -> p c t", t=T) for i in range(2)]
        gb_ap = [R1[:, 4096 + i * 2048: 4096 + (i + 1) * 2048].rearrange("p (c t) -> p c t", t=T) for i in range(2)]
        mix = R1[:, 8192:16384].rearrange("p (c t) -> p c t", t=T)
        oh_sb = R1[0:32, 0:2 * NCLS * GVW].bitcast(F32)
        gv_sb = R1[0:4, 2 * NCLS * GVW: 4 * NCLS * GVW].bitcast(F32)
        a_st = RZ[:, 0:8192].bitcast(F32).rearrange("p (c t) -> p c t", t=T)
        zsb = RZ[:, 0:8 * 514].rearrange("p (c t) -> p c t", t=514)
        bsb = RZ[:, 4112:4112 + 4096].rearrange("p (c t) -> p c t", t=T)
        yain = RZ[:, 8208:8208 + 4096].rearrange("p (c t) -> p c t", t=T)
        vst = VO[:, :].rearrange("p (b e) -> p b e", e=1024)
        onT = VO[:, :].rearrange("p (c t) -> p c t", t=T)

        regs = {}

        def emit(S, plan):
            B = Buf
            xh_b = [B(f"xh{c}") for c in range(KD)]
            xh_dma = B("xh_dma")
            xn_b = [B(f"xn{c}") for c in range(KD)]
            G_b = [B(f"G{j}") for j in range(KF)]
            wslots = [(wr_t[i], B(f"wr{i}")) for i in range(5)]
            conv_ev = {k: [] for k in wsrc}

            def resolve(key):
                name, n0, NC = key
                ev = None
                for (c0, c1, e_) in conv_ev[name]:
                    if c0 < n0 + NC and n0 < c1:
                        ev = e_
                return ev
            ws = WStream(S, wslots, plan, resolve)
            stg_ring = Ring([(stg_t[i], [B(f"stg{i}_{k}") for k in range(4)], B(f"stg{i}d")) for i in range(3)])
            tmp_ring = Ring([(tmp_t[i][:], B(f"tmp{i}")) for i in range(5)])
            acc_b, rstd_b = B("acc"), B("rstd")
            ast_b = [B(f"ast{i}") for i in range(8)]
            vst_b = [B(f"vst{i}") for i in range(16)]
            vst_dma = B("vst_dma")
            onT_b = [B(f"onT{i}") for i in range(8)]
            btl_ring = Ring([(btl_t[i], B(f"btl{i}")) for i in range(2)])
            pTb_b = B("pTb")
            of_b = [B("of0"), B("of1")]
            og_b = [B("og0"), B("og1")]
            const_b = B("consts")
            cst_b = B("cst")
            ones_b = B("ones")
            bank_b = [B(f"bank{i}") for i in range(8)]
            gen_ring = Ring([(bank_t[i], bank_b[i]) for i in range(8)])
            s_ring = Ring([(bank_t[i], bank_b[i]) for i in range(4)])
            qsb_b = B("qsb")
            kv_ring = Ring([(kvK[i], kvV[i], B(f"kvK{i}"), B(f"kvV{i}")) for i in range(3)])
            E_ring = Ring([(E_ap[i], B(f"E{i}")) for i in range(4)])
            acs_b = [B(f"acs{i}") for i in range(4)]
            ga_ring = Ring([(ga_ap[i], B(f"ga{i}")) for i in range(2)])
            gb_ring = Ring([(gb_ap[i], B(f"gb{i}")) for i in range(2)])
            mix_b = [B(f"mix{i}") for i in range(KD)]
            zsb_b, bsb_b = B("zsb"), B("bsb")
            yain_b = [B(f"yain{i}") for i in range(8)]
            oh_b, gv_b = B("oh"), B("gv")
            Hs_b = [B(f"Hs{t}") for t in range(NT)]
            Zs_b = [[B(f"Zs{t}_{i}") for i in range(2)] for t in range(NT)]
            Bs_b = [[B(f"Bs{t}_{i}") for i in range(2)] for t in range(NT)]
            QT_b = [[B(f"QT{t}_{i}") for i in range(2)] for t in range(NT)]
            KT_b = [[B(f"KT{t}_{i}") for i in range(2)] for t in range(NT)]
            Vl_b = [B(f"Vl{t}") for t in range(NT)]
            GA_b = [[B(f"GA{t}_{i}") for i in range(4)] for t in range(NT)]
            GB_b = [[B(f"GB{t}_{i}") for i in range(4)] for t in range(NT)]
            zedge_b, zedge_all_b, GV_b = B("zedge"), B("zall"), B("GV")
            KTall_b = [B(f"KTall{t}") for t in range(NT)]
            Vall_b = [B(f"Vall{t}") for t in range(NT)]
            BT_b = [B(f"BT{i}") for i in range(32)]
            NCH = 2
            conv_chains = [B(f"convchain{i}") for i in range(NCH)]
            conv_sems = [B(f"convsem{i}") for i in range(NCH)]
            conv_chain, conv_sem, coll_sem = conv_chains[0], conv_sems[0], B("collsem")
            conv_n = [0]
            out_evs = []
            dbg_evs = []

            def tmp():
                return tmp_ring.next()

            def bank():
                return gen_ring.next()

            def sp_setup(e):
                for i, (n, mx) in enumerate((("rk1", 3), ("rk2", 3), ("rk3", 3), ("zl", 7), ("zr", 7))):
                    r = e.alloc_register("r_" + n)
                    e.reg_load(r, offs_d[0:1, i:i + 1])
                    regs[n] = e.snap(r, donate=True, min_val=0, max_val=mx)
                return None

            def pool_setup(e):
                for i, n in enumerate(("pk1", "pk2", "pk3")):
                    r = e.alloc_register("r_" + n)
                    e.reg_load(r, offs_d[0:1, i:i + 1])
                    regs[n] = e.snap(r, donate=True, min_val=0, max_val=3)
                return None

            def act_setup(e):
                for i, n in enumerate(("rv1", "rv2", "rv3")):
                    r = e.alloc_register("r_" + n)
                    e.reg_load(r, offs_d[0:1, i:i + 1])
                    regs[n] = e.snap(r, donate=True, min_val=0, max_val=3)
                return None
            S.op("sp", sp_setup)

            ev = None
            for dst, src in ((gains, gains_d), (small, small_d), (relb, relb_d), (sel, sel_d), (tab32, tab_d)):
                ev = S.dma("sp", lambda e, dst=dst, src=src: e.dma_start(out=dst[:], in_=src[:, :]), const_b, batch=True)
            const_b.writer = ev
            S.op("pool", lambda e: e.memset(ones[:], 1.0), writes=[ones_b])
            S.op("pool", lambda e: e.memset(cst[:], 0.0), writes=[cst_b])
            S.op("pool", lambda e: e.memset(cst[:, 0:1], EPS), writes=[cst_b])
            S.op("dve", lambda e: e.tensor_scalar(out=cst[:, 2:3], in0=small[:, 24:25], scalar1=float(128 ** -0.5), scalar2=None, op0=ALU.mult), reads=[const_b, cst_b], writes=[cst_b])
            S.op("dve", lambda e: e.tensor_copy(out=cst[:, 3:4], in_=small[:, 25:26]), reads=[const_b, cst_b], writes=[cst_b])
            S.op("dve", lambda e: e.tensor_scalar(out=cst[:, 4:6], in0=small[:, 30:32], scalar1=float(1.0 - LAM_INIT), scalar2=None, op0=ALU.mult), reads=[const_b, cst_b], writes=[cst_b])
            S.op("dve", lambda e: e.tensor_copy(out=cst[:, 8:12], in_=relb[:, 60:64]), reads=[const_b, cst_b], writes=[cst_b])
            S.op("dve", lambda e: e.tensor_copy(out=cst[:, 12:16], in_=relb[:, 124:128]), reads=[const_b, cst_b], writes=[cst_b])
            S.op("dve", lambda e: e.tensor_tensor(out=cst[:, 16:20], in0=cst[:, 12:16], in1=cst[:, 8:12], op=ALU.subtract), reads=[cst_b], writes=[cst_b])
            for sp_ in range(3):
                S.op("dve", lambda e, sp_=sp_: e.scalar_tensor_tensor(out=cst[:, 20 + 4 * sp_:24 + 4 * sp_], in0=cst[:, 16:20], scalar=sel[:, sp_:sp_ + 1], in1=cst[:, 8:12], op0=ALU.mult, op1=ALU.add), reads=[cst_b, const_b], writes=[cst_b])
            S.op("dve", lambda e: e.tensor_tensor(out=cst[:, 6:7], in0=small[:, 26:27], in1=small[:, 27:28], op=ALU.mult), reads=[const_b, cst_b], writes=[cst_b])
            S.op("dve", lambda e: e.tensor_tensor(out=cst[:, 7:8], in0=small[:, 28:29], in1=small[:, 29:30], op=ALU.mult), reads=[const_b, cst_b], writes=[cst_b])
            bk, bkb = bank()
            S.op("pe", lambda e, bk=bk: e.matmul(bk[:, 0:2], lhsT=ones[:], rhs=cst[:, 6:8], start=True, stop=True), reads=[ones_b, cst_b], writes=[bkb])
            S.op("act", lambda e, bk=bk: e.activation(out=cst[:, 6:8], in_=bk[:, 0:2], func=AF.Exp), reads=[bkb, cst_b], writes=[cst_b])
            S.op("dve", lambda e: e.tensor_tensor(out=cst[:, 1:2], in0=cst[:, 6:7], in1=cst[:, 7:8], op=ALU.subtract), reads=[cst_b], writes=[cst_b])
            S.op("dve", lambda e: e.tensor_scalar(out=cst[:, 1:2], in0=cst[:, 1:2], scalar1=float(LAM_INIT), scalar2=None, op0=ALU.add), reads=[cst_b], writes=[cst_b])

            def conv_piece(name, c0, c1):
                src, dst = wsrc[name], wbf[name]
                ci = conv_n[0] % NCH
                conv_n[0] += 1
                ev = S.dma("pool", lambda e, src=src, dst=dst, c0=c0, c1=c1: e.dma_start(out=dst[:, c0:c1], in_=src[:, c0:c1]),
                           conv_sems[ci], writes=[conv_chains[ci]])
                conv_ev[name].append((c0, c1, ev))

            def conv_cols(names, width):
                ncols = wsrc[names[0]].shape[1]
                for c0 in range(0, ncols, width):
                    for n in names:
                        conv_piece(n, c0, min(ncols, c0 + width))

            conv_cols(["w1a", "w3a"], 512)
            conv_cols(["w2a"], 256)
            conv_cols(["win"], 1024)
            conv_cols(["wg"], 1024)
            late_pieces = []

            def late_cols(names, width):
                ncols = wsrc[names[0]].shape[1]
                for c0 in range(0, ncols, width):
                    for n in names:
                        late_pieces.append((n, c0, min(ncols, c0 + width)))

            late_cols(["wa"], 2048)
            late_cols(["wb"], 2048)
            late_cols(["wo"], 1024)
            late_cols(["wpp"], 2048)
            late_cols(["wpg"], 1024)
            n_small = len(late_pieces)
            late_cols(["w1b", "w3b"], 1024)
            late_cols(["w2b"], 512)

            def late_conv(n, after_ev):
                for _ in range(n):
                    if late_pieces:
                        name, c0, c1 = late_pieces.pop(0)
                        src, dst = wsrc[name], wbf[name]
                        ci = conv_n[0] % NCH
                        conv_n[0] += 1
                        ev = S.dma("pool", lambda e, src=src, dst=dst, c0=c0, c1=c1: e.dma_start(out=dst[:, c0:c1], in_=src[:, c0:c1]),
                                   conv_sems[ci], writes=[conv_chains[ci]], extra=[after_ev])
                        conv_ev[name].append((c0, c1, ev))

            def wget(tag, name, k0, KC, n0, NC):
                src = wbf[name][k0 * P:(k0 + KC) * P, n0:n0 + NC].rearrange("(k p) n -> p k n", p=P)
                return ws.get(tag, src, KC, NC, (name, n0, NC))

            def mm(bk, lhsT, rhs, start, stop):
                return lambda e: e.matmul(bk, lhsT=lhsT, rhs=rhs, start=start, stop=stop)

            attn_bufs = [qsb_b] + [x for it in kv_ring.items for x in it[2:4]] + [it[1] for it in E_ring.items] + acs_b
            mixer_bufs = [it[1] for it in ga_ring.items] + [it[1] for it in gb_ring.items] + mix_b
            setup_bufs = [oh_b, gv_b]
            S.dma("sp", lambda e: e.dma_start(out=oh_sb, in_=oh_d[:, :]), oh_b, writes=[oh_b])
            NG = NCLS * GVW // 480
            for i in range(NG):
                bk, bkb = bank()
                S.op("pe", mm(bk[0:4, 0:480], tab32[:, :], oh_sb[:, i * 480:(i + 1) * 480], True, True), reads=[oh_b, const_b], writes=[bkb])
                S.op("act", lambda e, bk=bk, i=i: e.activation(out=gv_sb[:, i * 480:(i + 1) * 480], in_=bk[0:4, 0:480], func=AF.Copy), reads=[bkb], writes=[gv_b])
            S.dma("sp", lambda e: e.dma_start(out=GV[:, :], in_=gv_sb), gv_b, reads=[gv_b], writes=[GV_b])
            S.fence(setup_bufs, G_b)

            def bias_unit(cls, h):
                hk, hkb = btl_ring.next()
                hap = bass.AP(GV.tensor, h * NCLS * GVW + cls * GVW, [[1, P], [1, T]])
                S.dma("sp", lambda e: e.dma_start(out=hk[:], in_=hap), hkb, reads=[GV_b], writes=[hkb])
                rev = bass.AP(hk[:].tensor, hk[:, T - 1:T].offset, [list(hk[:].ap[0]), [-1, T]])
                tl, tlb = tmp()
                S.op("dve", lambda e: e.tensor_copy(out=tl, in_=rev), reads=[hkb], writes=[tlb])
                S.dma("sp", lambda e: e.dma_start(out=BT[cls * 4 + h], in_=tl), tlb, reads=[tlb], writes=[BT_b[cls * 4 + h]])
                if cls in (0, 5):
                    ccol = (8 + h) if cls == 5 else (12 + h)
                    vcol = 4 if cls == 5 else 3
                    n = (24 + h) if cls == 5 else (28 + h)
                    t2, t2b = tmp()
                    S.op("dve", lambda e: e.tensor_scalar(out=t2, in0=tl, scalar1=cst[:, ccol:ccol + 1], scalar2=sel[:, vcol:vcol + 1], op0=ALU.subtract, op1=ALU.mult),
                         reads=[tlb, cst_b, const_b], writes=[t2b])
                    S.op("dve", lambda e: e.tensor_scalar(out=t2, in0=t2, scalar1=cst[:, ccol:ccol + 1], scalar2=None, op0=ALU.add), reads=[t2b, cst_b], writes=[t2b])
                    S.dma("sp", lambda e: e.dma_start(out=BT[n], in_=t2), t2b, reads=[t2b], writes=[BT_b[n]])

            bias_todo = [(cls, h) for cls in range(NCLS) for h in range(NH)]

            def bias_some(n):
                for _ in range(n):
                    if bias_todo:
                        bias_unit(*bias_todo.pop(0))

            def rms_stats():
                for c in range(KD):
                    sq, sqb = tmp()
                    S.op("act", lambda e, sq=sq, c=c: e.activation(out=sq, in_=xh[:, c, :], func=AF.Square), reads=[xh_b[c]], writes=[sqb])
                    if c == 0:
                        S.op("dve", lambda e, sq=sq: e.tensor_copy(out=acc[:], in_=sq), reads=[sqb], writes=[acc_b])
                    else:
                        S.op("dve", lambda e, sq=sq: e.tensor_tensor(out=acc[:], in0=acc[:], in1=sq, op=ALU.add), reads=[sqb, acc_b], writes=[acc_b])
                bk, bkb = bank()
                S.op("pe", mm(bk[:], ones[:], acc[:], True, True), reads=[acc_b, ones_b], writes=[bkb])
                S.op("act", lambda e, bk=bk: e.activation(out=rstd[:], in_=bk[:], func=AF.Sqrt, bias=cst[:, 0:1], scale=1.0 / D), reads=[bkb, cst_b], writes=[rstd_b])
                S.op("dve", lambda e: e.reciprocal(out=rstd[:], in_=rstd[:]), reads=[rstd_b], writes=[rstd_b])

            def rms_apply(gcol0):
                for c in range(KD):
                    S.op("dve", lambda e, c=c: e.scalar_tensor_tensor(out=xn[:, c, :], in0=xh[:, c, :], scalar=gains[:, gcol0 + c:gcol0 + c + 1], in1=rstd[:], op0=ALU.mult, op1=ALU.mult),
                         reads=[xh_b[c], rstd_b, const_b], writes=[xn_b[c]])

            def rmsnorm(gcol0):
                rms_stats()
                rms_apply(gcol0)

            def ffn(sfx):
                w1n, w3n, w2n = "w1" + sfx, "w3" + sfx, "w2" + sfx
                for jp in range(KF // 2):
                    w1t, w1b_ = wget((w1n, jp), w1n, 0, KD, jp * 256, 256)
                    w3t, w3b_ = wget((w3n, jp), w3n, 0, KD, jp * 256, 256)
                    for jj in range(2):
                        j = 2 * jp + jj
                        b1, b1b = bank()
                        b3, b3b = bank()
                        for k in range(KD):
                            S.op("pe", mm(b1[:], w1t[:, k, jj * P:(jj + 1) * P], xn[:, k, :], k == 0, k == KD - 1), reads=[w1b_, xn_b[k]], writes=[b1b])
                        for k in range(KD):
                            S.op("pe", mm(b3[:], w3t[:, k, jj * P:(jj + 1) * P], xn[:, k, :], k == 0, k == KD - 1), reads=[w3b_, xn_b[k]], writes=[b3b])
                        s, sb_ = tmp()
                        S.op("act", lambda e, s=s, b1=b1: e.activation(out=s, in_=b1[:], func=AF.Silu), reads=[b1b], writes=[sb_])
                        S.op("dve", lambda e, s=s, b3=b3, j=j: e.tensor_tensor(out=G[:, j, :], in0=s, in1=b3[:], op=ALU.mult), reads=[sb_, b3b], writes=[G_b[j]])
                for mp in range(KD // 2):
                    bm = [bank(), bank()]
                    for kg in range(3):
                        KC = 16 if kg < 2 else KF - 32
                        wt, wtb = wget((w2n, mp, kg), w2n, kg * 16, KC, mp * 256, 256)
                        for m_ in range(2):
                            for k in range(KC):
                                j = kg * 16 + k
                                S.op("pe", mm(bm[m_][0][:], wt[:, k, m_ * P:(m_ + 1) * P], G[:, j, :], j == 0, j == KF - 1), reads=[wtb, G_b[j]], writes=[bm[m_][1]])
                    for m_ in range(2):
                        m = 2 * mp + m_
                        S.op("dve", lambda e, bk=bm[m_][0], m=m: e.scalar_tensor_tensor(out=xh[:, m, :], in0=bk[:], scalar=0.5, in1=xh[:, m, :], op0=ALU.mult, op1=ALU.add),
                             reads=[bm[m_][1], xh_b[m]], writes=[xh_b[m]])

            def fm(X, c0, n, t0, w=T):
                return X[c0 * P:(c0 + n) * P, t0:t0 + w].rearrange("(c p) t -> p c t", p=P)

            class Stage:
                def __init__(self):
                    self.ap, self.cb, self.db = stg_ring.next()
                    self.k = 0

                def chunk(self):
                    k = self.k
                    self.k += 1
                    return self.ap[:, k, :], self.cb[k]

                def flush(self, dst, dst_buf):
                    ap = self.ap
                    return S.dma("sp", lambda e: e.dma_start(out=dst, in_=ap[:]), self.db, reads=self.cb, writes=[dst_buf])

            KTrot_b = [[B(f"KTrot{i}_{j}") for j in range(2)] for i in range(3)]
            Vrot_b = [[B(f"Vrot{i}_{j}") for j in range(2)] for i in range(3)]
            zhalo_b = B("zhalo")
            KTall_f = KTall.rearrange("t r n k -> t r (n k)")
            Vall_f = Vall.rearrange("t r n e -> t r (n e)")
            KTrot_f = KTrot.rearrange("s t n k -> s t (n k)")
            Vrot_f = Vrot.rearrange("s t n e -> s t (n e)")
            ROT_SPLIT = 0

            def rot_copies(queue, part, pref, segs=(0, 1, 2)):
                lo, hi = (0, ROT_SPLIT) if part == 0 else (ROT_SPLIT, NT)
                for i in segs:
                    S.dma(queue, lambda e, i=i: e.dma_start(out=KTrot_f[i:i + 1, lo:hi, :], in_=KTall_f[lo:hi, bass.ds(regs[pref + "%d" % (i + 1)], 1), :].rearrange("t o m -> o t m")),
                          KTrot_b[i][part], reads=KTall_b[lo:hi], writes=[KTrot_b[i][part]])
                    S.dma(queue, lambda e, i=i: e.dma_start(out=Vrot_f[i:i + 1, lo:hi, :], in_=Vall_f[lo:hi, bass.ds(regs[pref + "%d" % (i + 1)], 1), :].rearrange("t o m -> o t m")),
                          Vrot_b[i][part], reads=Vall_b[lo:hi], writes=[Vrot_b[i][part]])

            x_ev = [None]

            def phase1_tile(t):
                t0 = t * T
                if t == 0:
                    x_ev[0] = S.dma("sp", lambda e: e.dma_start(out=xh[:], in_=fm(xT, 0, KD, t0)), xh_dma, writes=xh_b)
                    rms_stats()
                else:
                    late_conv({1: 2, 2: 2, 3: 3, 4: 4, 5: 4, 6: 4, 7: 4}.get(t, 0), x_ev[0])
                rms_apply(0)
                bias_some(1)
                ffn("a")
                bias_some(1)
                S.dma("sp", lambda e: e.dma_start(out=fm(Hs, 0, KD, t0), in_=xh[:]), xh_dma, reads=xh_b, writes=[Hs_b[t]])
                rmsnorm(16)
                if t + 1 < nt1:
                    x_ev[0] = S.dma("sp", lambda e: e.dma_start(out=xh[:], in_=fm(xT, 0, KD, t0 + T)), xh_dma, writes=xh_b)
                bias_some(1)
                stage = None
                for wi in range(20):
                    wt, wtb = wget(("win", wi), "win", 0, KD, wi * 256, 256)
                    for jj in range(2):
                        ch = 2 * wi + jj
                        fam, i = ch // 8, ch % 8
                        bk, bkb = bank()
                        for k in range(KD):
                            S.op("pe", mm(bk[:], wt[:, k, jj * P:(jj + 1) * P], xn[:, k, :], k == 0, k == KD - 1), reads=[wtb, xn_b[k]], writes=[bkb])
                        if fam == 0:
                            S.op("act", lambda e, bk=bk, i=i: e.activation(out=a_st[:, i, :], in_=bk[:], func=AF.Copy), reads=[bkb], writes=[ast_b[i]])
                            continue
                        if i % 4 == 0:
                            stage = Stage()
                        dst, dstb = stage.chunk()
                        if fam == 1:
                            S.op("dve", lambda e, bk=bk, i=i, dst=dst: e.tensor_tensor(out=dst, in0=bk[:], in1=a_st[:, i, :], op=ALU.mult), reads=[bkb, ast_b[i]], writes=[dstb])
                        elif fam == 2:
                            S.op("act", lambda e, bk=bk, dst=dst: e.activation(out=dst, in_=bk[:], func=AF.Copy), reads=[bkb], writes=[dstb])
                        else:
                            sq, sqb = tmp()
                            S.op("act", lambda e, sq=sq, bk=bk: e.activation(out=sq, in_=bk[:], func=AF.Square), reads=[bkb], writes=[sqb])
                            b2, b2b = bank()
                            S.op("pe", mm(b2[:], ones[:], sq, True, True), reads=[sqb, ones_b], writes=[b2b])
                            r, rb = tmp()
                            S.op("act", lambda e, r=r, b2=b2: e.activation(out=r, in_=b2[:], func=AF.Sqrt, bias=cst[:, 0:1], scale=1.0 / 128), reads=[b2b, cst_b], writes=[rb])
                            S.op("dve", lambda e, r=r: e.reciprocal(out=r, in_=r), reads=[rb], writes=[rb])
                            gcol = 2 if fam == 3 else 3
                            S.op("dve", lambda e, bk=bk, r=r, dst=dst, gcol=gcol: e.scalar_tensor_tensor(out=dst, in0=bk[:], scalar=cst[:, gcol:gcol + 1], in1=r, op0=ALU.mult, op1=ALU.mult),
                                 reads=[bkb, rb, cst_b], writes=[dstb])
                        if i % 4 == 3:
                            half = i // 4
                            if fam == 4:
                                stage.flush(KTl[t, half * 512:(half + 1) * 512, :].rearrange("(c p) k -> p c k", p=P), KT_b[t][half])
                            else:
                                X, Xb = {1: (Zs, Zs_b), 2: (Bs, Bs_b), 3: (QTs, QT_b)}[fam]
                                stage.flush(fm(X, half * 4, 4, t0), Xb[t][half])
                if t + 1 < nt1:
                    rms_stats()
                for hv in range(NH):
                    wt, wtb = wget(("win", 20 + hv), "win", 0, KD, 5120 + hv * 256, 256)
                    for tb in range(4):
                        bk, bkb = bank()
                        for k in range(KD):
                            S.op("pe", mm(bk[:, 0:256], xn[:, k, tb * P:(tb + 1) * P], wt[:, k, :], k == 0, k == KD - 1), reads=[wtb, xn_b[k]], writes=[bkb])
                        vb = vst_b[tb * 4 + hv]
                        if (tb + hv) % 2 == 0:
                            S.op("act", lambda e, bk=bk, tb=tb, hv=hv: e.activation(out=vst[:, tb, hv * 256:(hv + 1) * 256], in_=bk[:, 0:256], func=AF.Copy), reads=[bkb], writes=[vb])
                        else:
                            S.op("dve", lambda e, bk=bk, tb=tb, hv=hv: e.tensor_copy(out=vst[:, tb, hv * 256:(hv + 1) * 256], in_=bk[:, 0:256]), reads=[bkb], writes=[vb])
                S.dma("sp", lambda e: e.dma_start(out=Vl[t].rearrange("(b p) e -> p b e", p=P), in_=vst[:, :, :]), vst_dma, reads=vst_b, writes=[Vl_b[t]])
                if 'exch' not in _SKIP:
                    grp = [[0, 1, 2, 3], [4, 5, 6, 7]]
                    S.coll(lambda e: e.collective_compute("AllGather", ALU.bypass, replica_groups=grp, ins=[KTl[t]], outs=[KTall[t].rearrange("r n k -> (r n) k")]), coll_sem,
                           reads=KT_b[t], writes=[KTall_b[t]])
                    S.coll(lambda e: e.collective_compute("AllGather", ALU.bypass, replica_groups=grp, ins=[Vl[t]], outs=[Vall[t].rearrange("r n e -> (r n) e")]), coll_sem,
                           reads=[Vl_b[t]], writes=[Vall_b[t]])
                    if t == ROT_SPLIT - 1:
                        S.op("pool", pool_setup)
                        rot_copies("pool", 0, "pk")
                bias_some(1 if t < NT - 1 else 100)
                for gi in range(16):
                    wt, wtb = wget(("wg", gi), "wg", 0, KD, gi * 256, 256)
                    for jj in range(2):
                        gc = 2 * gi + jj
                        bk, bkb = bank()
                        for k in range(KD):
                            S.op("pe", mm(bk[:], wt[:, k, jj * P:(jj + 1) * P], xn[:, k, :], k == 0, k == KD - 1), reads=[wtb, xn_b[k]], writes=[bkb])
                        if gc % 4 == 0:
                            stage = Stage()
                        dst, dstb = stage.chunk()
                        S.op("act", lambda e, bk=bk, dst=dst: e.activation(out=dst, in_=bk[:], func=AF.Sigmoid), reads=[bkb], writes=[dstb])
                        if gc % 4 == 3:
                            q4 = (gc % 16) // 4
                            if gc < 16:
                                stage.flush(fm(GAs, q4 * 4, 4, t0), GA_b[t][q4])
                            else:
                                stage.flush(fm(GBs, q4 * 4, 4, t0), GB_b[t][q4])

            for t in range(nt1):
                phase1_tile(t)

            if 'exch' not in _SKIP:
                zrd = [Zs_b[0][0], Zs_b[0][1], Zs_b[NT - 1][0], Zs_b[NT - 1][1]]
                if 'zedge' not in _SKIP:
                    S.dma("pool", lambda e: e.dma_start(out=zedge[0:1, :].rearrange("o (n u) -> (o n) u", u=1), in_=Zs[:, 0:1], allow_slow_non_contiguous=True), conv_sem, reads=zrd, writes=[conv_chain, zedge_b])
                    S.dma("pool", lambda e: e.dma_start(out=zedge[1:2, :].rearrange("o (n u) -> (o n) u", u=1), in_=Zs[:, TOK - 1:TOK], allow_slow_non_contiguous=True), conv_sem, reads=zrd, writes=[conv_chain, zedge_b])
                grp = [[0, 1, 2, 3], [4, 5, 6, 7]]
                if 'zcoll' not in _SKIP:
                    S.coll(lambda e: e.collective_compute("AllGather", ALU.bypass, replica_groups=grp, ins=[zedge[:, :]], outs=[zedge_all[:, :]]), coll_sem,
                           reads=[zedge_b], writes=[zedge_all_b])


            late_conv(1000, x_ev[0])
            if debug:
                dbg_evs.append(S.dma("pool", lambda e: e.dma_start(out=KTd[:, :, :], in_=KTl[:, :, :]), conv_sem, reads=[b for tb_ in KT_b for b in tb_], writes=[conv_chain]))
                dbg_evs.append(S.dma("pool", lambda e: e.dma_start(out=Vd[:, :, :], in_=Vl[:, :, :]), conv_sem, reads=Vl_b, writes=[conv_chain]))
            def bias_kind(qt, j, h):
                if j < 32:
                    delta = 128 * j - 512 * qt
                    if -128 <= delta <= 512:
                        return ("tile", ((delta + 128) // 128) * 4 + h)
                    return ("far", (8 + h) if delta < 0 else (12 + h))
                if qt == NT - 1 and j == 32:
                    return ("tile", 24 + h)
                if qt == 0 and j == 127:
                    return ("tile", 28 + h)
                sp_ = j // 32
                return ("far", 20 + 4 * (sp_ - 1) + h)

            rot_done = []

            def kv_load(h, sp_, kc):
                K, V, Kb, Vb = kv_ring.next()
                t2 = 2 * kc
                if sp_ == 1 and not rot_done:
                    rot_done.append(1)
                    rot_copies("sp", 1, "rk", (0,))
                if sp_ == 0:
                    ksrc = KTl[t2:t2 + 2, h * 256:(h + 1) * 256, :]
                    vsrc = Vl[t2:t2 + 2, :, h * 256:(h + 1) * 256]
                    kdep = KT_b[t2] + KT_b[t2 + 1]
                    vdep = [Vl_b[t2], Vl_b[t2 + 1]]
                else:
                    ksrc = KTrot[sp_ - 1, t2:t2 + 2, h * 256:(h + 1) * 256, :]
                    vsrc = Vrot[sp_ - 1, t2:t2 + 2, :, h * 256:(h + 1) * 256]
                    kdep = [KTrot_b[sp_ - 1][0 if t2 < ROT_SPLIT else 1]]
                    vdep = [Vrot_b[sp_ - 1][0 if t2 < ROT_SPLIT else 1]]
                for c in range(2):
                    Kd = K[:, c, :].rearrange("p (t k) -> p t k", t=2)
                    ks = ksrc[:, c * P:(c + 1) * P, :].rearrange("t p k -> p t k")
                    S.dma("sp", lambda e, Kd=Kd, ks=ks: e.dma_start(out=Kd, in_=ks), Kb, reads=kdep, writes=[Kb], batch=(c == 1))
                for tt in range(2):
                    Vd = V[:, tt * 4:(tt + 1) * 4, :]
                    vs = vsrc[tt].rearrange("(b p) e -> p b e", p=P)
                    S.dma("sp", lambda e, Vd=Vd, vs=vs: e.dma_start(out=Vd, in_=vs), Vb, reads=vdep, writes=[Vb], batch=(tt == 1))
                if sp_ >= 1 and len(rot_done) == sp_ and sp_ < 3:
                    rot_done.append(1)
                    rot_copies("sp", 1, "rk", (sp_,))
                    if sp_ == 2:
                        for i, rn in enumerate(("zl", "zr")):
                            S.dma("sp", lambda e, i=i, rn=rn: e.dma_start(out=zhalo[i:i + 1, :], in_=zedge_all[bass.ds(regs[rn], 1), :]),
                                  zhalo_b, reads=[zedge_all_b], writes=[zhalo_b])
                return K, V, Kb, Vb

            ATT_CHUNKS = [(h, sp_, kc) for h in range(NH) for sp_ in range(4) for kc in range(4)]
            att_pre = {}

            def attention_prefetch(qt):
                t0 = qt * T
                S.fence(G_b + mixer_bufs + setup_bufs, attn_bufs)
                S.dma("sp", lambda e: e.dma_start(out=qsb, in_=fm(QTs, 0, 8, t0)), qsb_b, reads=QT_b[qt], writes=[qsb_b])
                loaded = [kv_load(*ATT_CHUNKS[0]), kv_load(*ATT_CHUNKS[1])]
                att_pre[qt] = loaded

            def attention(qt, fill=()):
                fill = list(fill)
                t0 = qt * T
                if qt not in att_pre:
                    attention_prefetch(qt)
                pend_B = []
                chunks = ATT_CHUNKS
                loaded = att_pre.pop(qt)
                nl = len(loaded)
                for ci, (h, sp_, kc) in enumerate(chunks):
                    while nl < min(len(chunks), ci + 2):
                        loaded.append(kv_load(*chunks[nl]))
                        nl += 1
                    K, V, Kb, Vb = loaded[ci]
                    if sp_ == 0 and kc == 0:
                        pend = None
                        O = [[(bank_t[4 + 2 * c + ec], bank_b[4 + 2 * c + ec]) for ec in range(2)] for c in range(2)]
                    for blk in range(8):
                        j = sp_ * 32 + kc * 8 + blk
                        kind = bias_kind(qt, j, h)
                        bt = None
                        if kind[0] == "tile":
                            bt, btb = btl_ring.next()
                            S.dma("sp", lambda e, bt=bt, n=kind[1]: e.dma_start(out=bt[:], in_=BT[n]), btb, reads=[BT_b[kind[1]]], writes=[btb])
                        Es = []
                        for c in range(2):
                            sbk, sbkb = s_ring.next()
                            S.op("pe", mm(sbk[:], K[:, c, blk * P:(blk + 1) * P], qsb[:, 2 * h + c, :], True, True), reads=[Kb, qsb_b], writes=[sbkb])
                            E, Eb = E_ring.next()
                            if bt is None:
                                S.op("act", lambda e, E=E, sbk=sbk, col=kind[1]: e.activation(out=E, in_=sbk[:], func=AF.Exp, bias=cst[:, col:col + 1], scale=1.0), reads=[sbkb, cst_b], writes=[Eb])
                            else:
                                tl, tlb = tmp()
                                S.op("dve", lambda e, tl=tl, sbk=sbk, bt=bt: e.tensor_tensor(out=tl, in0=sbk[:], in1=bt[:], op=ALU.add), reads=[sbkb, btb], writes=[tlb])
                                S.op("act", lambda e, E=E, tl=tl: e.activation(out=E, in_=tl, func=AF.Exp), reads=[tlb], writes=[Eb])
                            a_i = 2 * c + (j % 2)
                            seng = "pool" if a_i in (1, 3) else "dve"
                            if j < 2:
                                S.op(seng, lambda e, E=E, a_i=a_i: e.tensor_copy(out=acs_ap[a_i], in_=E), reads=[Eb], writes=[acs_b[a_i]])
                            else:
                                S.op(seng, lambda e, E=E, a_i=a_i: e.tensor_tensor(out=acs_ap[a_i], in0=acs_ap[a_i], in1=E, op=ALU.add), reads=[Eb, acs_b[a_i]], writes=[acs_b[a_i]])
                            Es.append((E, Eb))
                        if pend is not None:
                            emit_pv(pend, O)
                        pend = (Es, V, Vb, blk, j)
                        if h == 1 and j >= 30 and (j - 30) % 10 == 0 and fill:
                            fill.pop(0)()
                        if j == 8 and pend_B:
                            attn_epilogue_B1(qt, pend_B[0])
                        if j == 20 and pend_B:
                            attn_epilogue_B(qt, pend_B.pop())
                    if sp_ == 3 and kc == 3:
                        emit_pv(pend, O)
                        pend = None
                        attn_epilogue_A(qt, h, O)
                        if h == NH - 1:
                            while fill:
                                fill.pop(0)()
                            attn_epilogue_B1(qt, h)
                            attn_epilogue_B(qt, h)
                        else:
                            pend_B.append(h)

            def emit_pv(pend, O):
                Es, V, Vb, blk, j = pend
                for c in range(2):
                    E, Eb = Es[c]
                    for ec in range(2):
                        S.op("pe", mm(O[c][ec][0][:], V[:, blk, ec * P:(ec + 1) * P], E, j == 0, j == 127), reads=[Vb, Eb], writes=[O[c][ec][1]])

            def attn_epilogue_A(qt, h, O):
                for ec in range(2):
                    S.op("act", lambda e, ec=ec: e.activation(out=of_t[ec][:], in_=O[0][ec][0][:], func=AF.Copy), reads=[O[0][ec][1]], writes=[of_b[ec]])
                    S.op("dve", lambda e, ec=ec: e.tensor_copy(out=og_t[ec][:], in_=O[1][ec][0][:]), reads=[O[1][ec][1]], writes=[og_b[ec]])
                S.op("dve", lambda e: e.tensor_tensor(out=acc[:], in0=acs_ap[0], in1=acs_ap[1], op=ALU.add), reads=[acs_b[0], acs_b[1]], writes=[acc_b])
                S.op("dve", lambda e: e.tensor_tensor(out=rstd[:], in0=acs_ap[2], in1=acs_ap[3], op=ALU.add), reads=[acs_b[2], acs_b[3]], writes=[rstd_b])

            def attn_epilogue_B1(qt, h):
                rr = []
                for c in range(2):
                    src, srcb = (acc, acc_b) if c == 0 else (rstd, rstd_b)
                    sbk, sbkb = s_ring.next()
                    S.op("pe", mm(sbk[:], ones[:], src[:], True, True), reads=[ones_b, srcb], writes=[sbkb])
                    r, rb = tmp()
                    S.op("dve", lambda e, r=r, sbk=sbk: e.reciprocal(out=r, in_=sbk[:]), reads=[sbkb], writes=[rb])
                    if c == 1:
                        S.op("dve", lambda e, r=r: e.tensor_scalar(out=r, in0=r, scalar1=cst[:, 1:2], scalar2=None, op0=ALU.mult), reads=[rb, cst_b], writes=[rb])
                    rr.append((r, rb))
                for ec in range(2):
                    S.op("dve", lambda e, ec=ec: e.tensor_tensor(out=og_t[ec][:], in0=og_t[ec][:], in1=rr[1][0], op=ALU.mult), reads=[og_b[ec], rr[1][1]], writes=[og_b[ec]])
                    S.op("dve", lambda e, ec=ec: e.tensor_tensor(out=of_t[ec][:], in0=of_t[ec][:], in1=rr[0][0], op=ALU.mult), reads=[of_b[ec], rr[0][1]], writes=[of_b[ec]])
                    S.op("dve", lambda e, ec=ec: e.tensor_tensor(out=of_t[ec][:], in0=of_t[ec][:], in1=og_t[ec][:], op=ALU.subtract), reads=[of_b[ec], og_b[ec]], writes=[of_b[ec]])

            def attn_epilogue_B(qt, h):
                for ec in range(2):
                    if ec == 0:
                        S.op("act", lambda e, ec=ec: e.activation(out=acc[:], in_=of_t[ec][:], func=AF.Square), reads=[of_b[ec]], writes=[acc_b])
                    else:
                        sq, sqb = tmp()
                        S.op("act", lambda e, sq=sq, ec=ec: e.activation(out=sq, in_=of_t[ec][:], func=AF.Square), reads=[of_b[ec]], writes=[sqb])
                        S.op("dve", lambda e, sq=sq: e.tensor_tensor(out=acc[:], in0=acc[:], in1=sq, op=ALU.add), reads=[sqb, acc_b], writes=[acc_b])
                sbk, sbkb = s_ring.next()
                S.op("pe", mm(sbk[:], ones[:], acc[:], True, True), reads=[ones_b, acc_b], writes=[sbkb])
                S.op("act", lambda e, sbk=sbk: e.activation(out=rstd[:], in_=sbk[:], func=AF.Sqrt, bias=cst[:, 0:1], scale=1.0 / 256), reads=[sbkb, cst_b], writes=[rstd_b])
                S.op("dve", lambda e: e.reciprocal(out=rstd[:], in_=rstd[:]), reads=[rstd_b], writes=[rstd_b])
                for ec in range(2):
                    S.op("dve", lambda e, ec=ec: e.scalar_tensor_tensor(out=onT[:, 2 * h + ec, :], in0=of_t[ec][:], scalar=cst[:, 4 + ec:5 + ec], in1=rstd[:], op0=ALU.mult, op1=ALU.mult),
                         reads=[of_b[ec], rstd_b, cst_b], writes=[onT_b[2 * h + ec]])

            def conv_loads(qt):
                t0 = qt * T
                lo = 1 if qt == 0 else 0
                hi = 513 if qt == NT - 1 else 514
                zr = []
                for tt in (qt - 1, qt, qt + 1):
                    if 0 <= tt < NT:
                        zr += Zs_b[tt]
                S.dma("sp", lambda e: e.dma_start(out=zsb[:, :, lo:hi], in_=fm(Zs, 0, 8, t0 - 1 + lo, hi - lo)), zsb_b, reads=zr, writes=[zsb_b])
                if qt == 0 or qt == NT - 1:
                    col = 0 if qt == 0 else 513
                    zi = 0 if qt == 0 else 1
                    vcol = 3 if qt == 0 else 4
                    S.dma("sp", lambda e: e.dma_start(out=zsb[:, :, col:col + 1], in_=zhalo[zi:zi + 1, :].rearrange("o (c p u) -> p (o c) u", p=P, u=1), allow_slow_non_contiguous=True),
                          zsb_b, reads=[zhalo_b], writes=[zsb_b])
                    S.op("dve", lambda e: e.tensor_scalar(out=zsb[:, :, col:col + 1], in0=zsb[:, :, col:col + 1], scalar1=sel[:, vcol:vcol + 1], scalar2=None, op0=ALU.mult), reads=[zsb_b, const_b], writes=[zsb_b])
                S.dma("sp", lambda e: e.dma_start(out=bsb, in_=fm(Bs, 0, 8, t0)), bsb_b, reads=Bs_b[qt], writes=[bsb_b])

            def conv_chunk(qt, ch):
                if True:
                    y, yb = tmp()
                    S.op("dve", lambda e, y=y, ch=ch: e.tensor_scalar(out=y, in0=zsb[:, ch, 0:T], scalar1=small[:, ch:ch + 1], scalar2=None, op0=ALU.mult), reads=[zsb_b, const_b], writes=[yb])
                    S.op("dve", lambda e, y=y, ch=ch: e.scalar_tensor_tensor(out=y, in0=zsb[:, ch, 1:T + 1], scalar=small[:, 8 + ch:9 + ch], in1=y, op0=ALU.mult, op1=ALU.add), reads=[zsb_b, const_b, yb], writes=[yb])
                    S.op("dve", lambda e, y=y, ch=ch: e.scalar_tensor_tensor(out=y, in0=zsb[:, ch, 2:T + 2], scalar=small[:, 16 + ch:17 + ch], in1=y, op0=ALU.mult, op1=ALU.add), reads=[zsb_b, const_b, yb], writes=[yb])
                    S.op("dve", lambda e, y=y, ch=ch: e.tensor_tensor(out=yain[:, ch, :], in0=y, in1=bsb[:, ch, :], op=ALU.mult), reads=[yb, bsb_b], writes=[yain_b[ch]])

            def mixer(qt):
                t0 = qt * T
                S.fence(attn_bufs + G_b, mixer_bufs)
                for mq in range(4):
                    ga, gab = ga_ring.next()
                    gb, gbb = gb_ring.next()
                    S.dma("sp", lambda e, ga=ga, mq=mq: e.dma_start(out=ga, in_=fm(GAs, mq * 4, 4, t0)), gab, reads=[GA_b[qt][mq]], writes=[gab])
                    S.dma("sp", lambda e, gb=gb, mq=mq: e.dma_start(out=gb, in_=fm(GBs, mq * 4, 4, t0)), gbb, reads=[GB_b[qt][mq]], writes=[gbb])
                    wat, wab = wget(("wa", mq), "wa", 0, 8, mq * 512, 512)
                    wbt, wbb = wget(("wb", mq), "wb", 0, 8, mq * 512, 512)
                    for mi in range(4):
                        m = mq * 4 + mi
                        ba, bab = bank()
                        bb_, bbb = bank()
                        for k in range(8):
                            S.op("pe", mm(ba[:], wat[:, k, mi * P:(mi + 1) * P], yain[:, k, :], k == 0, k == 7), reads=[wab, yain_b[k]], writes=[bab])
                        for k in range(8):
                            S.op("pe", mm(bb_[:], wbt[:, k, mi * P:(mi + 1) * P], onT[:, k, :], k == 0, k == 7), reads=[wbb, onT_b[k]], writes=[bbb])
                        ta, tab_ = tmp()
                        tb_, tbb = tmp()
                        S.op("dve", lambda e, ta=ta, ba=ba, ga=ga, mi=mi: e.tensor_tensor(out=ta, in0=ba[:], in1=ga[:, mi, :], op=ALU.mult), reads=[bab, gab], writes=[tab_])
                        S.op("dve", lambda e, tb_=tb_, bb_=bb_, gb=gb, mi=mi: e.tensor_tensor(out=tb_, in0=bb_[:], in1=gb[:, mi, :], op=ALU.mult), reads=[bbb, gbb], writes=[tbb])
                        S.op("dve", lambda e, ta=ta, tb_=tb_, m=m: e.tensor_tensor(out=mix[:, m, :], in0=ta, in1=tb_, op=ALU.add), reads=[tab_, tbb], writes=[mix_b[m]])
                if debug:
                    dbg_evs.append(S.dma("sp", lambda e: e.dma_start(out=fm(ONd, 0, 8, t0), in_=onT), vst_dma, reads=onT_b))
                    dbg_evs.append(S.dma("sp", lambda e: e.dma_start(out=fm(MIXd, 0, KD, t0), in_=mix), vst_dma, reads=mix_b))
                for wi in range(8):
                    wt, wtb = wget(("wo", wi), "wo", 0, KD, wi * 256, 256)
                    for jj in range(2):
                        m = 2 * wi + jj
                        bk, bkb = bank()
                        for k in range(KD):
                            S.op("pe", mm(bk[:], wt[:, k, jj * P:(jj + 1) * P], mix[:, k, :], k == 0, k == KD - 1), reads=[wtb, mix_b[k]], writes=[bkb])
                        S.op("dve", lambda e, bk=bk, m=m: e.tensor_tensor(out=xh[:, m, :], in0=bk[:], in1=xh[:, m, :], op=ALU.add), reads=[bkb, xh_b[m]], writes=[xh_b[m]])
                if debug:
                    dbg_evs.append(S.dma("sp", lambda e: e.dma_start(out=fm(H2d, 0, KD, t0), in_=xh[:]), xh_dma, reads=xh_b))

            def ple(qt):
                t0 = qt * T
                S.dma("pool", lambda e: e.dma_start(out=pTb[:], in_=fm(pT, 0, 2, t0)), pTb_b, writes=[pTb_b])
                for wi in range(8):
                    wt, wtb = wget(("wpg", wi), "wpg", 0, KD, wi * 256, 256)
                    wpt, wpb = wget(("wpp", wi), "wpp", 0, 2, wi * 256, 256)
                    for jj in range(2):
                        m = 2 * wi + jj
                        bg, bgb = bank()
                        bp, bpb = bank()
                        for k in range(KD):
                            S.op("pe", mm(bg[:], wt[:, k, jj * P:(jj + 1) * P], xn[:, k, :], k == 0, k == KD - 1), reads=[wtb, xn_b[k]], writes=[bgb])
                        for k in range(2):
                            S.op("pe", mm(bp[:], wpt[:, k, jj * P:(jj + 1) * P], pTb[:, k, :], k == 0, k == 1), reads=[wpb, pTb_b], writes=[bpb])
                        sg, sgb = tmp()
                        S.op("act", lambda e, sg=sg, bg=bg: e.activation(out=sg, in_=bg[:], func=AF.Sigmoid), reads=[bgb], writes=[sgb])
                        S.op("dve", lambda e, sg=sg, bp=bp: e.tensor_tensor(out=sg, in0=sg, in1=bp[:], op=ALU.mult), reads=[sgb, bpb], writes=[sgb])
                        S.op("dve", lambda e, sg=sg, m=m: e.tensor_tensor(out=xh[:, m, :], in0=sg, in1=xh[:, m, :], op=ALU.add), reads=[sgb, xh_b[m]], writes=[xh_b[m]])
                out_evs.append(S.dma("sp", lambda e: e.dma_start(out=fm(outT, 0, KD, t0), in_=xh[:]), xh_dma, reads=xh_b))

            S.fence(ast_b, [zsb_b, bsb_b] + yain_b)
            S.fence(vst_b, onT_b)
            for qt in range(nt2):
                if qt == 0:
                    attention(qt)
                    S.dma("sp", lambda e, qt=qt: e.dma_start(out=xh[:], in_=fm(Hs, 0, KD, qt * T)), xh_dma, reads=[Hs_b[qt]], writes=xh_b)
                    conv_loads(qt)
                    for ch in range(8):
                        conv_chunk(qt, ch)
                else:
                    conv_loads(qt)
                    attention(qt, [(lambda qt=qt, ch=ch: conv_chunk(qt, ch)) for ch in range(8)])
                    S.dma("sp", lambda e, qt=qt: e.dma_start(out=xh[:], in_=fm(Hs, 0, KD, qt * T)), xh_dma, reads=[Hs_b[qt]], writes=xh_b)
                mixer(qt)
                rmsnorm(32)
                S.fence(attn_bufs + mixer_bufs, G_b)
                ffn("b")
                if qt + 1 < nt2:
                    attention_prefetch(qt + 1)
                rmsnorm(48)
                ple(qt)

            S.op("sp", None, extra=out_evs + dbg_evs)
            return ws.plan

        plan = emit(NullSched(), None)
        S = Sched(nc)
        emit(S, plan)
        stats = S.emit(st)
        print("sched stats (n_instr, n_waits):", stats, "sems:", len(S.sems) + 5, flush=True)
    return nc


def _prep_shared(inputs):
    f = lambda a: np.ascontiguousarray(np.asarray(a, dtype=np.float32))
    sh = {}
    for k in ("ffn1_w1", "ffn1_w3", "ffn1_w2", "w_in", "w_gate", "w_branch_a", "w_branch_b", "w_out",
              "ffn2_w1", "ffn2_w3", "ffn2_w2", "w_ple_gate", "w_ple_proj"):
        sh[k] = f(inputs[k][0])
    g = np.zeros((P, 64), np.float32)
    for i, k in enumerate(("ffn1_norm", "mix_norm", "ffn2_norm", "ple_norm")):
        g[:, 16 * i:16 * (i + 1)] = f(inputs[k][0]).reshape(KD, P).T
    sh["gains"] = g
    sm = np.zeros((P, 40), np.float32)
    cw = f(inputs["conv_w"][0])
    for tap in range(3):
        sm[:, tap * 8:(tap + 1) * 8] = cw[tap].reshape(8, P).T
    sm[:, 24] = f(inputs["q_norm"][0])
    sm[:, 25] = f(inputs["k_norm"][0])
    sm[:, 26] = f(inputs["lam_q1"][0])
    sm[:, 27] = f(inputs["lam_k1"][0])
    sm[:, 28] = f(inputs["lam_q2"][0])
    sm[:, 29] = f(inputs["lam_k2"][0])
    sm[:, 30:32] = f(inputs["sub_norm"][0]).reshape(2, P).T
    sh["small"] = sm
    rb = f(inputs["rel_bias"])
    sh["tab32"] = rb
    sh["relb"] = np.ascontiguousarray(np.broadcast_to(rb.reshape(1, 128), (P, 128)))
    sh["onehot"] = _onehot_const()
    return sh


def _prep_core(inputs, c):
    b, r = c // 4, c % 4
    s0 = r * TOK
    m = {}
    m["xT"] = np.ascontiguousarray(np.asarray(inputs["x"][b, s0:s0 + TOK, :], dtype=np.float32).T)
    m["pT"] = np.ascontiguousarray(np.asarray(inputs["p"][0, b, s0:s0 + TOK, :], dtype=np.float32).T)
    sel = np.zeros((P, 8), np.float32)
    for sp_ in range(1, 4):
        sel[:, sp_ - 1] = 1.0 if r + sp_ < 4 else 0.0
    sel[:, 3] = 1.0 if r > 0 else 0.0
    sel[:, 4] = 1.0 if r < 3 else 0.0
    m["sel"] = sel
    offs = np.zeros((1, 16), np.int32)
    for sp_ in range(1, 4):
        offs[0, sp_ - 1] = (r + sp_) % 4
    offs[0, 3] = 2 * ((r - 1) % 4) + 1
    offs[0, 4] = 2 * ((r + 1) % 4)
    m["offs"] = offs
    return m


_NC_CACHE = {}
_SKIP = set()
_DBGSET = set()


def kernel(**inputs):
    if "nc" not in _NC_CACHE:
        _NC_CACHE["nc"] = build_program(False)
    nc = _NC_CACHE["nc"]
    sh = _prep_shared(inputs)
    in_maps = []
    for c in range(8):
        m = dict(sh)
        m.update(_prep_core(inputs, c))
        in_maps.append(m)
    res = run_bass_kernel_spmd(nc, in_maps, core_ids=list(range(8)))
    out = np.empty((2, 4 * TOK, D), np.float32)
    for c in range(8):
        b, r = c // 4, c % 4
        out[b, r * TOK:(r + 1) * TOK, :] = np.asarray(res.results[c]["outT"], dtype=np.float32).T
    return out
```

```python
import math
from contextlib import ExitStack

import numpy as np
import concourse.bass as bass
import concourse.mybir as mybir
from concourse.bass_utils import run_bass_kernel_spmd

F32 = mybir.dt.float32
BF16 = mybir.dt.bfloat16
I32 = mybir.dt.int32
AF = mybir.ActivationFunctionType
ALU = mybir.AluOpType

P = 128
D = 2048
KD = 16
FF = 5632
KF = 44
T = 512
TOK = 4096
NT = TOK // T
CW = 1024
AW = 1024
NH = 4
PLE = 256
EPS = 1e-6
LAM_INIT = 0.8 - 0.6 * math.exp(-0.3 * 0)
NCLS = 6
GVW = 640

ENGS = ("pe", "act", "dve", "pool", "sp")


class Buf:
    __slots__ = ("name", "writer", "readers", "sem")

    def __init__(self, name):
        self.name = name
        self.writer = None
        self.readers = {}
        self.sem = None


class SemSlot:
    __slots__ = ("handle", "count", "name")

    def __init__(self, name):
        self.name = name
        self.handle = None
        self.count = 0


class Ins:
    __slots__ = ("fn", "deps", "signaled", "cum", "dma_sem", "inc")

    def __init__(self, fn, deps):
        self.fn = fn
        self.deps = deps
        self.signaled = False
        self.cum = 0
        self.dma_sem = None
        self.inc = 16


class NullSched:
    null = True

    def op(self, *a, **k):
        return None

    def dma(self, *a, **k):
        return None

    def coll(self, *a, **k):
        return None

    def fence(self, *a, **k):
        return None


class Sched:
    null = False

    def __init__(self, nc):
        self.nc = nc
        self.streams = {e: [] for e in ENGS}
        self.sems = []
        self.eng_sem = {e: SemSlot("eng_" + e) for e in ENGS}

    def _deps(self, eng, reads, writes, extra):
        deps = set()
        for b in reads:
            if b.writer is not None:
                deps.add(b.writer)
        for b in writes:
            if b.writer is not None:
                deps.add(b.writer)
            deps.update(b.readers.values())
        for e in extra:
            if e is not None:
                deps.add(e)
        if eng == "pe":
            deps = {d for d in deps if not (d[0] == "E" and d[1] == "pe")}
        return deps

    def op(self, eng, fn, reads=(), writes=(), extra=()):
        deps = self._deps(eng, reads, writes, extra)
        st = self.streams[eng]
        ev = ("E", eng, len(st))
        st.append(Ins(fn, deps))
        for b in reads:
            b.readers[eng] = ev
        for b in writes:
            b.writer = ev
            b.readers = {}
        return ev

    def _slot(self, sem_buf):
        if sem_buf.sem is None:
            sem_buf.sem = SemSlot("d_" + sem_buf.name)
            self.sems.append(sem_buf.sem)
        return sem_buf.sem

    def dma(self, queue, fn, sem_buf, reads=(), writes=(), extra=(), batch=False):
        deps = self._deps("dma", reads, writes, extra)
        slot = self._slot(sem_buf)
        if batch:
            deps = {d for d in deps if not (d[0] == "D" and d[1] is slot)}
        slot.count += 16
        ins = Ins(fn, deps)
        ins.dma_sem = slot
        self.streams[queue].append(ins)
        ev = ("D", slot, slot.count)
        for b in reads:
            b.readers[("dma", id(slot))] = ev
        for b in writes:
            b.writer = ev
            b.readers = {}
        return ev

    def coll(self, fn, sem_buf, reads=(), writes=()):
        deps = self._deps("dma", reads, writes, ())
        slot = self._slot(sem_buf)
        slot.count += 1
        ins = Ins(fn, deps)
        ins.dma_sem = slot
        ins.inc = 1
        self.streams["pool"].append(ins)
        ev = ("D", slot, slot.count)
        for b in reads:
            b.readers[("dma", id(slot))] = ev
        for b in writes:
            b.writer = ev
            b.readers = {}
        return ev

    def fence(self, old_bufs, new_bufs):
        evs = []
        for b in old_bufs:
            if b.writer is not None:
                evs.append(b.writer)
            evs.extend(b.readers.values())
        evs = list(set(evs))
        self.nfence = getattr(self, "nfence", 0) + 1
        for b in new_bufs:
            for i, ev in enumerate(evs):
                b.readers[("fence", self.nfence, i)] = ev

    def emit(self, stack):
        nc = self.nc
        for e in ENGS:
            for ins in self.streams[e]:
                for d in ins.deps:
                    if d[0] == "E":
                        self.streams[d[1]][d[2]].signaled = True
        for e in ENGS:
            c = 0
            for ins in self.streams[e]:
                if ins.signaled:
                    c += 1
                ins.cum = c
        for e in ENGS:
            self.eng_sem[e].handle = stack.enter_context(nc.semaphore("sem_" + e))
        for s in self.sems:
            s.handle = stack.enter_context(nc.semaphore(s.name))
        block = stack.enter_context(nc.Block())
        stats = {}

        def run(engname, eng):
            waited = {}
            nw = 0
            for ins in self.streams[engname]:
                need = {}
                for d in ins.deps:
                    if d[0] == "E":
                        slot = self.eng_sem[d[1]]
                        cnt = self.streams[d[1]][d[2]].cum
                    else:
                        slot = d[1]
                        cnt = d[2]
                    if need.get(slot, 0) < cnt:
                        need[slot] = cnt
                for slot, cnt in need.items():
                    if waited.get(slot, 0) >= cnt:
                        continue
                    eng.wait_ge(slot.handle, cnt)
                    waited[slot] = cnt
                    nw += 1
                if ins.fn is None:
                    continue
                bi = ins.fn(eng)
                if bi is None:
                    assert not ins.signaled and ins.dma_sem is None
                    continue
                if ins.dma_sem is not None:
                    if ins.inc == 1:
                        bi.then_inc(ins.dma_sem.handle)
                    else:
                        bi.then_inc(ins.dma_sem.handle, 16)
                elif ins.signaled:
                    bi.then_inc(self.eng_sem[engname].handle, 1)
            stats[engname] = (len(self.streams[engname]), nw)

        @block.tensor
        def _(pe):
            run("pe", pe)

        @block.scalar
        def _(act):
            run("act", act)

        @block.vector
        def _(dve):
            run("dve", dve)

        @block.gpsimd
        def _(pool):
            run("pool", pool)

        @block.sync
        def _(sp):
            run("sp", sp)

        return stats


def _t5_bucket_np(rel):
    nb = 16
    max_exact = 8
    rel = np.asarray(rel, dtype=np.int64)
    ret = np.where(rel > 0, nb, 0).astype(np.int64)
    n = np.abs(rel)
    nf = np.maximum(n, 1).astype(np.float32)
    val = (np.log(nf / np.float32(max_exact)) / np.float32(math.log(128 / max_exact))
           * np.float32(nb - max_exact)).astype(np.float32)
    large = max_exact + val.astype(np.int32).astype(np.int64)
    large = np.minimum(large, nb - 1)
    return ret + np.where(n < max_exact, n, large)


def _onehot_const():
    oh = np.zeros((32, NCLS * GVW), dtype=np.float32)
    for cls in range(NCLS):
        delta = (cls - 1) * 128
        m = np.arange(GVW)
        bk = _t5_bucket_np(delta + m - 511)
        oh[bk, cls * GVW + m] = 1.0
    return oh


class Ring:
    def __init__(self, items):
        self.items = items
        self.i = 0

    def next(self):
        it = self.items[self.i % len(self.items)]
        self.i += 1
        return it


class WStream:
    def __init__(self, S, slots, plan, resolve):
        self.S = S
        self.slots = slots
        self.R = len(slots)
        self.record = plan is None
        self.plan = [] if plan is None else plan
        self.resolve = resolve
        self.next_load = 0
        self.next_use = 0

    def _load(self, m):
        tag, src, KC, NC, key = self.plan[m]
        ap, buf = self.slots[m % self.R]
        dst = ap[:, 0:KC * NC].rearrange("p (k n) -> p k n", n=NC)
        self.S.dma("sp", lambda e, dst=dst, src=src: e.dma_start(out=dst, in_=src), buf, writes=[buf], extra=[self.resolve(key)])

    def get(self, tag, src, KC, NC, key):
        n = self.next_use
        self.next_use += 1
        ap, buf = self.slots[n % self.R]
        view = ap[:, 0:KC * NC].rearrange("p (k n) -> p k n", n=NC)
        if self.record:
            self.plan.append((tag, src, KC, NC, key))
            return view, buf
        assert self.plan[n][0] == tag, (self.plan[n][0], tag)
        lim = min(len(self.plan), n + self.R - 1)
        while self.next_load < lim:
            self._load(self.next_load)
            self.next_load += 1
        return view, buf


def build_program(debug=False, nt1=NT, nt2=NT):
    nc = bass.Bass("TRN2", target_bir_lowering=False)

    def din(name, shape, dt=F32):
        return nc.dram_tensor(name, list(shape), dt, kind="ExternalInput").ap()

    def dscr(name, shape, dt, dbg=False):
        if dbg and debug and (not _DBGSET or name in _DBGSET):
            return nc.dram_tensor(name, list(shape), dt, kind="ExternalOutput").ap()
        return nc.dram_tensor(name, list(shape), dt).ap()

    xT = din("xT", [D, TOK])
    pT = din("pT", [PLE, TOK])
    wsrc = {
        "w1a": din("ffn1_w1", [D, FF]), "w3a": din("ffn1_w3", [D, FF]), "w2a": din("ffn1_w2", [FF, D]),
        "win": din("w_in", [D, 6144]), "wg": din("w_gate", [D, 2 * D]),
        "wa": din("w_branch_a", [CW, D]), "wb": din("w_branch_b", [AW, D]), "wo": din("w_out", [D, D]),
        "w1b": din("ffn2_w1", [D, FF]), "w3b": din("ffn2_w3", [D, FF]), "w2b": din("ffn2_w2", [FF, D]),
        "wpg": din("w_ple_gate", [D, D]), "wpp": din("w_ple_proj", [PLE, D]),
    }
    gains_d = din("gains", [P, 64])
    small_d = din("small", [P, 40])
    relb_d = din("relb", [P, 128])
    tab_d = din("tab32", [32, 4])
    oh_d = din("onehot", [32, NCLS * GVW])
    sel_d = din("sel", [P, 8])
    offs_d = din("offs", [1, 16], I32)
    outT = nc.dram_tensor("outT", [D, TOK], F32, kind="ExternalOutput").ap()

    wbf = {k: dscr("b16_" + k, v.shape, BF16) for k, v in wsrc.items()}
    Hs = dscr("Hs", [D, TOK], F32, dbg=True)
    Zs = dscr("Zs", [CW, TOK], BF16, dbg=True)
    Bs = dscr("Bs", [CW, TOK], BF16, dbg=True)
    QTs = dscr("QTs", [AW, TOK], BF16, dbg=True)
    GAs = dscr("GAs", [D, TOK], BF16, dbg=True)
    GBs = dscr("GBs", [D, TOK], BF16, dbg=True)
    KTl = dscr("KTl", [NT, AW, T], BF16)
    Vl = dscr("Vl", [NT, T, AW], BF16)
    KTall = dscr("KTall", [NT, 4, AW, T], BF16)
    Vall = dscr("Vall", [NT, 4, T, AW], BF16)
    KTrot = dscr("KTrot", [3, NT, AW, T], BF16)
    Vrot = dscr("Vrot", [3, NT, T, AW], BF16)
    zhalo = dscr("zhalo", [2, CW], BF16)
    zedge = dscr("zedge", [2, CW], BF16)
    zedge_all = dscr("zedge_all", [8, CW], BF16)
    GV = dscr("GV", [4, NCLS * GVW], F32)
    BT = dscr("BT", [32, P, T], F32, dbg=True)
    KTd = dscr("KTd", [NT, AW, T], BF16, dbg=True) if debug else None
    Vd = dscr("Vd", [NT, T, AW], BF16, dbg=True) if debug else None
    ONd = dscr("ONd", [AW, TOK], BF16, dbg=True) if debug else None
    MIXd = dscr("MIXd", [D, TOK], BF16, dbg=True) if debug else None
    H2d = dscr("H2d", [D, TOK], F32, dbg=True) if debug else None

    st = ExitStack()
    with st:
        def sb(name, shape, dt):
            return st.enter_context(nc.sbuf_tensor(name, list(shape), dt))

        xh = sb("xh", [P, KD, T], F32)
        xn = sb("xn", [P, KD, T], BF16)
        R1 = sb("R1", [P, KF * T], BF16)
        wr_t = [sb(f"wr{i}", [P, 4096], BF16) for i in range(5)]
        stg_t = [sb(f"stg{i}", [P, 4, T], BF16) for i in range(3)]
        tmp_t = [sb(f"tmp{i}", [P, T], F32) for i in range(5)]
        acc = sb("acc", [P, T], F32)
        rstd = sb("rstd", [P, T], F32)
        RZ = sb("RZ", [P, 12416], BF16)
        VO = sb("VO", [P, 4 * 1024], BF16)
        btl_t = [sb(f"btl{i}", [P, T], F32) for i in range(2)]
        pTb = sb("pTb", [P, 2, T], BF16)
        of_t = [sb(f"of{i}", [P, T], F32) for i in range(2)]
        og_t = [sb(f"og{i}", [P, T], F32) for i in range(2)]
        gains = sb("gains_s", [P, 64], F32)
        small = sb("small_s", [P, 40], F32)
        relb = sb("relb_s", [P, 128], F32)
        sel = sb("sel_s", [P, 8], F32)
        ones = sb("ones_s", [P, P], F32)
        cst = sb("cst_s", [P, 32], F32)
        tab32 = sb("tab32_s", [32, 4], F32)
        bank_t = [st.enter_context(nc.psum_tensor(f"bank{i}", [P, T], F32)) for i in range(8)]

        G = R1[:, :].rearrange("p (j t) -> p j t", t=T)
        qsb = R1[:, 0:4096].rearrange("p (c t) -> p c t", t=T)
        kvK = [R1[:, 4096 + i * 4096: 4096 + i * 4096 + 2048].rearrange("p (c t) -> p c t", t=1024) for i in range(3)]
        kvV = [R1[:, 4096 + i * 4096 + 2048: 4096 + (i + 1) * 4096].rearrange("p (b e) -> p b e", e=256) for i in range(3)]
        E_ap = [R1[:, 16384 + i * T: 16384 + (i + 1) * T] for i in range(4)]
        acs_ap = [R1[:, 18432 + i * 1024: 18432 + (i + 1) * 1024].bitcast(F32) for i in range(4)]
        ga_ap = [R1[:, i * 2048:(i + 1) * 2048].rearrange("p (c t) -> p c t", t=T) for i in range(2)]
        gb_ap = [R1[:, 4096 + i * 2048: 4096 + (i + 1) * 2048].rearrange("p (c t) -> p c t", t=T) for i in range(2)]
        mix = R1[:, 8192:16384].rearrange("p (c t) -> p c t", t=T)
        oh_sb = R1[0:32, 0:2 * NCLS * GVW].bitcast(F32)
        gv_sb = R1[0:4, 2 * NCLS * GVW: 4 * NCLS * GVW].bitcast(F32)
        a_st = RZ[:, 0:8192].bitcast(F32).rearrange("p (c t) -> p c t", t=T)
        zsb = RZ[:, 0:8 * 514].rearrange("p (c t) -> p c t", t=514)
        bsb = RZ[:, 4112:4112 + 4096].rearrange("p (c t) -> p c t", t=T)
        yain = RZ[:, 8208:8208 + 4096].rearrange("p (c t) -> p c t", t=T)
        vst = VO[:, :].rearrange("p (b e) -> p b e", e=1024)
        onT = VO[:, :].rearrange("p (c t) -> p c t", t=T)

        regs = {}

        def emit(S, plan):
            B = Buf
            xh_b = [B(f"xh{c}") for c in range(KD)]
            xh_dma = B("xh_dma")
            xn_b = [B(f"xn{c}") for c in range(KD)]
            G_b = [B(f"G{j}") for j in range(KF)]
            wslots = [(wr_t[i], B(f"wr{i}")) for i in range(5)]
            conv_ev = {k: [] for k in wsrc}

            def resolve(key):
                name, n0, NC = key
                ev = None
                for (c0, c1, e_) in conv_ev[name]:
                    if c0 < n0 + NC and n0 < c1:
                        ev = e_
                return ev
            ws = WStream(S, wslots, plan, resolve)
            stg_ring = Ring([(stg_t[i], [B(f"stg{i}_{k}") for k in range(4)], B(f"stg{i}d")) for i in range(3)])
            tmp_ring = Ring([(tmp_t[i][:], B(f"tmp{i}")) for i in range(5)])
            acc_b, rstd_b = B("acc"), B("rstd")
            ast_b = [B(f"ast{i}") for i in range(8)]
            vst_b = [B(f"vst{i}") for i in range(16)]
            vst_dma = B("vst_dma")
            onT_b = [B(f"onT{i}") for i in range(8)]
            btl_ring = Ring([(btl_t[i], B(f"btl{i}")) for i in range(2)])
            pTb_b = B("pTb")
            of_b = [B("of0"), B("of1")]
            og_b = [B("og0"), B("og1")]
            const_b = B("consts")
            cst_b = B("cst")
            ones_b = B("ones")
            bank_b = [B(f"bank{i}") for i in range(8)]
            gen_ring = Ring([(bank_t[i], bank_b[i]) for i in range(8)])
            s_ring = Ring([(bank_t[i], bank_b[i]) for i in range(4)])
            qsb_b = B("qsb")
            kv_ring = Ring([(kvK[i], kvV[i], B(f"kvK{i}"), B(f"kvV{i}")) for i in range(3)])
            E_ring = Ring([(E_ap[i], B(f"E{i}")) for i in range(4)])
            acs_b = [B(f"acs{i}") for i in range(4)]
            ga_ring = Ring([(ga_ap[i], B(f"ga{i}")) for i in range(2)])
            gb_ring = Ring([(gb_ap[i], B(f"gb{i}")) for i in range(2)])
            mix_b = [B(f"mix{i}") for i in range(KD)]
            zsb_b, bsb_b = B("zsb"), B("bsb")
            yain_b = [B(f"yain{i}") for i in range(8)]
            oh_b, gv_b = B("oh"), B("gv")
            Hs_b = [B(f"Hs{t}") for t in range(NT)]
            Zs_b = [[B(f"Zs{t}_{i}") for i in range(2)] for t in range(NT)]
            Bs_b = [[B(f"Bs{t}_{i}") for i in range(2)] for t in range(NT)]
            QT_b = [[B(f"QT{t}_{i}") for i in range(2)] for t in range(NT)]
            KT_b = [[B(f"KT{t}_{i}") for i in range(2)] for t in range(NT)]
            Vl_b = [B(f"Vl{t}") for t in range(NT)]
            GA_b = [[B(f"GA{t}_{i}") for i in range(4)] for t in range(NT)]
            GB_b = [[B(f"GB{t}_{i}") for i in range(4)] for t in range(NT)]
            zedge_b, zedge_all_b, GV_b = B("zedge"), B("zall"), B("GV")
            KTall_b = [B(f"KTall{t}") for t in range(NT)]
            Vall_b = [B(f"Vall{t}") for t in range(NT)]
            BT_b = [B(f"BT{i}") for i in range(32)]
            NCH = 2
            conv_chains = [B(f"convchain{i}") for i in range(NCH)]
            conv_sems = [B(f"convsem{i}") for i in range(NCH)]
            conv_chain, conv_sem, coll_sem = conv_chains[0], conv_sems[0], B("collsem")
            conv_n = [0]
            out_evs = []
            dbg_evs = []

            def tmp():
                return tmp_ring.next()

            def bank():
                return gen_ring.next()

            def sp_setup(e):
                for i, (n, mx) in enumerate((("rk1", 3), ("rk2", 3), ("rk3", 3), ("zl", 7), ("zr", 7))):
                    r = e.alloc_register("r_" + n)
                    e.reg_load(r, offs_d[0:1, i:i + 1])
                    regs[n] = e.snap(r, donate=True, min_val=0, max_val=mx)
                return None

            def pool_setup(e):
                for i, n in enumerate(("pk1", "pk2", "pk3")):
                    r = e.alloc_register("r_" + n)
                    e.reg_load(r, offs_d[0:1, i:i + 1])
                    regs[n] = e.snap(r, donate=True, min_val=0, max_val=3)
                return None

            def act_setup(e):
                for i, n in enumerate(("rv1", "rv2", "rv3")):
                    r = e.alloc_register("r_" + n)
                    e.reg_load(r, offs_d[0:1, i:i + 1])
                    regs[n] = e.snap(r, donate=True, min_val=0, max_val=3)
                return None
            S.op("sp", sp_setup)

            ev = None
            for dst, src in ((gains, gains_d), (small, small_d), (relb, relb_d), (sel, sel_d), (tab32, tab_d)):
                ev = S.dma("sp", lambda e, dst=dst, src=src: e.dma_start(out=dst[:], in_=src[:, :]), const_b, batch=True)
            const_b.writer = ev
            S.op("pool", lambda e: e.memset(ones[:], 1.0), writes=[ones_b])
            S.op("pool", lambda e: e.memset(cst[:], 0.0), writes=[cst_b])
            S.op("pool", lambda e: e.memset(cst[:, 0:1], EPS), writes=[cst_b])
            S.op("dve", lambda e: e.tensor_scalar(out=cst[:, 2:3], in0=small[:, 24:25], scalar1=float(128 ** -0.5), scalar2=None, op0=ALU.mult), reads=[const_b, cst_b], writes=[cst_b])
            S.op("dve", lambda e: e.tensor_copy(out=cst[:, 3:4], in_=small[:, 25:26]), reads=[const_b, cst_b], writes=[cst_b])
            S.op("dve", lambda e: e.tensor_scalar(out=cst[:, 4:6], in0=small[:, 30:32], scalar1=float(1.0 - LAM_INIT), scalar2=None, op0=ALU.mult), reads=[const_b, cst_b], writes=[cst_b])
            S.op("dve", lambda e: e.tensor_copy(out=cst[:, 8:12], in_=relb[:, 60:64]), reads=[const_b, cst_b], writes=[cst_b])
            S.op("dve", lambda e: e.tensor_copy(out=cst[:, 12:16], in_=relb[:, 124:128]), reads=[const_b, cst_b], writes=[cst_b])
            S.op("dve", lambda e: e.tensor_tensor(out=cst[:, 16:20], in0=cst[:, 12:16], in1=cst[:, 8:12], op=ALU.subtract), reads=[cst_b], writes=[cst_b])
            for sp_ in range(3):
                S.op("dve", lambda e, sp_=sp_: e.scalar_tensor_tensor(out=cst[:, 20 + 4 * sp_:24 + 4 * sp_], in0=cst[:, 16:20], scalar=sel[:, sp_:sp_ + 1], in1=cst[:, 8:12], op0=ALU.mult, op1=ALU.add), reads=[cst_b, const_b], writes=[cst_b])
            S.op("dve", lambda e: e.tensor_tensor(out=cst[:, 6:7], in0=small[:, 26:27], in1=small[:, 27:28], op=ALU.mult), reads=[const_b, cst_b], writes=[cst_b])
            S.op("dve", lambda e: e.tensor_tensor(out=cst[:, 7:8], in0=small[:, 28:29], in1=small[:, 29:30], op=ALU.mult), reads=[const_b, cst_b], writes=[cst_b])
            bk, bkb = bank()
            S.op("pe", lambda e, bk=bk: e.matmul(bk[:, 0:2], lhsT=ones[:], rhs=cst[:, 6:8], start=True, stop=True), reads=[ones_b, cst_b], writes=[bkb])
            S.op("act", lambda e, bk=bk: e.activation(out=cst[:, 6:8], in_=bk[:, 0:2], func=AF.Exp), reads=[bkb, cst_b], writes=[cst_b])
            S.op("dve", lambda e: e.tensor_tensor(out=cst[:, 1:2], in0=cst[:, 6:7], in1=cst[:, 7:8], op=ALU.subtract), reads=[cst_b], writes=[cst_b])
            S.op("dve", lambda e: e.tensor_scalar(out=cst[:, 1:2], in0=cst[:, 1:2], scalar1=float(LAM_INIT), scalar2=None, op0=ALU.add), reads=[cst_b], writes=[cst_b])

            def conv_piece(name, c0, c1):
                src, dst = wsrc[name], wbf[name]
                ci = conv_n[0] % NCH
                conv_n[0] += 1
                ev = S.dma("pool", lambda e, src=src, dst=dst, c0=c0, c1=c1: e.dma_start(out=dst[:, c0:c1], in_=src[:, c0:c1]),
                           conv_sems[ci], writes=[conv_chains[ci]])
                conv_ev[name].append((c0, c1, ev))

            def conv_cols(names, width):
                ncols = wsrc[names[0]].shape[1]
                for c0 in range(0, ncols, width):
                    for n in names:
                        conv_piece(n, c0, min(ncols, c0 + width))

            conv_cols(["w1a", "w3a"], 512)
            conv_cols(["w2a"], 256)
            conv_cols(["win"], 1024)
            conv_cols(["wg"], 1024)
            late_pieces = []

            def late_cols(names, width):
                ncols = wsrc[names[0]].shape[1]
                for c0 in range(0, ncols, width):
                    for n in names:
                        late_pieces.append((n, c0, min(ncols, c0 + width)))

            late_cols(["wa"], 2048)
            late_cols(["wb"], 2048)
            late_cols(["wo"], 1024)
            late_cols(["wpp"], 2048)
            late_cols(["wpg"], 1024)
            n_small = len(late_pieces)
            late_cols(["w1b", "w3b"], 1024)
            late_cols(["w2b"], 512)

            def late_conv(n, after_ev):
                for _ in range(n):
                    if late_pieces:
                        name, c0, c1 = late_pieces.pop(0)
                        src, dst = wsrc[name], wbf[name]
                        ci = conv_n[0] % NCH
                        conv_n[0] += 1
                        ev = S.dma("pool", lambda e, src=src, dst=dst, c0=c0, c1=c1: e.dma_start(out=dst[:, c0:c1], in_=src[:, c0:c1]),
                                   conv_sems[ci], writes=[conv_chains[ci]], extra=[after_ev])
                        conv_ev[name].append((c0, c1, ev))

            def wget(tag, name, k0, KC, n0, NC):
                src = wbf[name][k0 * P:(k0 + KC) * P, n0:n0 + NC].rearrange("(k p) n -> p k n", p=P)
                return ws.get(tag, src, KC, NC, (name, n0, NC))

            def mm(bk, lhsT, rhs, start, stop):
                return lambda e: e.matmul(bk, lhsT=lhsT, rhs=rhs, start=start, stop=stop)

            attn_bufs = [qsb_b] + [x for it in kv_ring.items for x in it[2:4]] + [it[1] for it in E_ring.items] + acs_b
            mixer_bufs = [it[1] for it in ga_ring.items] + [it[1] for it in gb_ring.items] + mix_b
            setup_bufs = [oh_b, gv_b]
            S.dma("sp", lambda e: e.dma_start(out=oh_sb, in_=oh_d[:, :]), oh_b, writes=[oh_b])
            NG = NCLS * GVW // 480
            for i in range(NG):
                bk, bkb = bank()
                S.op("pe", mm(bk[0:4, 0:480], tab32[:, :], oh_sb[:, i * 480:(i + 1) * 480], True, True), reads=[oh_b, const_b], writes=[bkb])
                S.op("act", lambda e, bk=bk, i=i: e.activation(out=gv_sb[:, i * 480:(i + 1) * 480], in_=bk[0:4, 0:480], func=AF.Copy), reads=[bkb], writes=[gv_b])
            S.dma("sp", lambda e: e.dma_start(out=GV[:, :], in_=gv_sb), gv_b, reads=[gv_b], writes=[GV_b])
            S.fence(setup_bufs, G_b)

            def bias_unit(cls, h):
                hk, hkb = btl_ring.next()
                hap = bass.AP(GV.tensor, h * NCLS * GVW + cls * GVW, [[1, P], [1, T]])
                S.dma("sp", lambda e: e.dma_start(out=hk[:], in_=hap), hkb, reads=[GV_b], writes=[hkb])
                rev = bass.AP(hk[:].tensor, hk[:, T - 1:T].offset, [list(hk[:].ap[0]), [-1, T]])
                tl, tlb = tmp()
                S.op("dve", lambda e: e.tensor_copy(out=tl, in_=rev), reads=[hkb], writes=[tlb])
                S.dma("sp", lambda e: e.dma_start(out=BT[cls * 4 + h], in_=tl), tlb, reads=[tlb], writes=[BT_b[cls * 4 + h]])
                if cls in (0, 5):
                    ccol = (8 + h) if cls == 5 else (12 + h)
                    vcol = 4 if cls == 5 else 3
                    n = (24 + h) if cls == 5 else (28 + h)
                    t2, t2b = tmp()
                    S.op("dve", lambda e: e.tensor_scalar(out=t2, in0=tl, scalar1=cst[:, ccol:ccol + 1], scalar2=sel[:, vcol:vcol + 1], op0=ALU.subtract, op1=ALU.mult),
                         reads=[tlb, cst_b, const_b], writes=[t2b])
                    S.op("dve", lambda e: e.tensor_scalar(out=t2, in0=t2, scalar1=cst[:, ccol:ccol + 1], scalar2=None, op0=ALU.add), reads=[t2b, cst_b], writes=[t2b])
                    S.dma("sp", lambda e: e.dma_start(out=BT[n], in_=t2), t2b, reads=[t2b], writes=[BT_b[n]])

            bias_todo = [(cls, h) for cls in range(NCLS) for h in range(NH)]

            def bias_some(n):
                for _ in range(n):
                    if bias_todo:
                        bias_unit(*bias_todo.pop(0))

            def rms_stats():
                for c in range(KD):
                    sq, sqb = tmp()
                    S.op("act", lambda e, sq=sq, c=c: e.activation(out=sq, in_=xh[:, c, :], func=AF.Square), reads=[xh_b[c]], writes=[sqb])
                    if c == 0:
                        S.op("dve", lambda e, sq=sq: e.tensor_copy(out=acc[:], in_=sq), reads=[sqb], writes=[acc_b])
                    else:
                        S.op("dve", lambda e, sq=sq: e.tensor_tensor(out=acc[:], in0=acc[:], in1=sq, op=ALU.add), reads=[sqb, acc_b], writes=[acc_b])
                bk, bkb = bank()
                S.op("pe", mm(bk[:], ones[:], acc[:], True, True), reads=[acc_b, ones_b], writes=[bkb])
                S.op("act", lambda e, bk=bk: e.activation(out=rstd[:], in_=bk[:], func=AF.Sqrt, bias=cst[:, 0:1], scale=1.0 / D), reads=[bkb, cst_b], writes=[rstd_b])
                S.op("dve", lambda e: e.reciprocal(out=rstd[:], in_=rstd[:]), reads=[rstd_b], writes=[rstd_b])

            def rms_apply(gcol0):
                for c in range(KD):
                    S.op("dve", lambda e, c=c: e.scalar_tensor_tensor(out=xn[:, c, :], in0=xh[:, c, :], scalar=gains[:, gcol0 + c:gcol0 + c + 1], in1=rstd[:], op0=ALU.mult, op1=ALU.mult),
                         reads=[xh_b[c], rstd_b, const_b], writes=[xn_b[c]])

            def rmsnorm(gcol0):
                rms_stats()
                rms_apply(gcol0)

            def ffn(sfx):
                w1n, w3n, w2n = "w1" + sfx, "w3" + sfx, "w2" + sfx
                for jp in range(KF // 2):
                    w1t, w1b_ = wget((w1n, jp), w1n, 0, KD, jp * 256, 256)
                    w3t, w3b_ = wget((w3n, jp), w3n, 0, KD, jp * 256, 256)
                    for jj in range(2):
                        j = 2 * jp + jj
                        b1, b1b = bank()
                        b3, b3b = bank()
                        for k in range(KD):
                            S.op("pe", mm(b1[:], w1t[:, k, jj * P:(jj + 1) * P], xn[:, k, :], k == 0, k == KD - 1), reads=[w1b_, xn_b[k]], writes=[b1b])
                        for k in range(KD):
                            S.op("pe", mm(b3[:], w3t[:, k, jj * P:(jj + 1) * P], xn[:, k, :], k == 0, k == KD - 1), reads=[w3b_, xn_b[k]], writes=[b3b])
                        s, sb_ = tmp()
                        S.op("act", lambda e, s=s, b1=b1: e.activation(out=s, in_=b1[:], func=AF.Silu), reads=[b1b], writes=[sb_])
                        S.op("dve", lambda e, s=s, b3=b3, j=j: e.tensor_tensor(out=G[:, j, :], in0=s, in1=b3[:], op=ALU.mult), reads=[sb_, b3b], writes=[G_b[j]])
                for mp in range(KD // 2):
                    bm = [bank(), bank()]
                    for kg in range(3):
                        KC = 16 if kg < 2 else KF - 32
                        wt, wtb = wget((w2n, mp, kg), w2n, kg * 16, KC, mp * 256, 256)
                        for m_ in range(2):
                            for k in range(KC):
                                j = kg * 16 + k
                                S.op("pe", mm(bm[m_][0][:], wt[:, k, m_ * P:(m_ + 1) * P], G[:, j, :], j == 0, j == KF - 1), reads=[wtb, G_b[j]], writes=[bm[m_][1]])
                    for m_ in range(2):
                        m = 2 * mp + m_
                        S.op("dve", lambda e, bk=bm[m_][0], m=m: e.scalar_tensor_tensor(out=xh[:, m, :], in0=bk[:], scalar=0.5, in1=xh[:, m, :], op0=ALU.mult, op1=ALU.add),
                             reads=[bm[m_][1], xh_b[m]], writes=[xh_b[m]])

            def fm(X, c0, n, t0, w=T):
                return X[c0 * P:(c0 + n) * P, t0:t0 + w].rearrange("(c p) t -> p c t", p=P)

            class Stage:
                def __init__(self):
                    self.ap, self.cb, self.db = stg_ring.next()
                    self.k = 0

                def chunk(self):
                    k = self.k
                    self.k += 1
                    return self.ap[:, k, :], self.cb[k]

                def flush(self, dst, dst_buf):
                    ap = self.ap
                    return S.dma("sp", lambda e: e.dma_start(out=dst, in_=ap[:]), self.db, reads=self.cb, writes=[dst_buf])

            KTrot_b = [[B(f"KTrot{i}_{j}") for j in range(2)] for i in range(3)]
            Vrot_b = [[B(f"Vrot{i}_{j}") for j in range(2)] for i in range(3)]
            zhalo_b = B("zhalo")
            KTall_f = KTall.rearrange("t r n k -> t r (n k)")
            Vall_f = Vall.rearrange("t r n e -> t r (n e)")
            KTrot_f = KTrot.rearrange("s t n k -> s t (n k)")
            Vrot_f = Vrot.rearrange("s t n e -> s t (n e)")
            ROT_SPLIT = 0

            def rot_copies(queue, part, pref, segs=(0, 1, 2)):
                lo, hi = (0, ROT_SPLIT) if part == 0 else (ROT_SPLIT, NT)
                for i in segs:
                    S.dma(queue, lambda e, i=i: e.dma_start(out=KTrot_f[i:i + 1, lo:hi, :], in_=KTall_f[lo:hi, bass.ds(regs[pref + "%d" % (i + 1)], 1), :].rearrange("t o m -> o t m")),
                          KTrot_b[i][part], reads=KTall_b[lo:hi], writes=[KTrot_b[i][part]])
                    S.dma(queue, lambda e, i=i: e.dma_start(out=Vrot_f[i:i + 1, lo:hi, :], in_=Vall_f[lo:hi, bass.ds(regs[pref + "%d" % (i + 1)], 1), :].rearrange("t o m -> o t m")),
                          Vrot_b[i][part], reads=Vall_b[lo:hi], writes=[Vrot_b[i][part]])

            x_ev = [None]

            def phase1_tile(t):
                t0 = t * T
                if t == 0:
                    x_ev[0] = S.dma("sp", lambda e: e.dma_start(out=xh[:], in_=fm(xT, 0, KD, t0)), xh_dma, writes=xh_b)
                    rms_stats()
                else:
                    late_conv({1: 2, 2: 2, 3: 3, 4: 4, 5: 4, 6: 4, 7: 4}.get(t, 0), x_ev[0])
                rms_apply(0)
                bias_some(1)
                ffn("a")
                bias_some(1)
                S.dma("sp", lambda e: e.dma_start(out=fm(Hs, 0, KD, t0), in_=xh[:]), xh_dma, reads=xh_b, writes=[Hs_b[t]])
                rmsnorm(16)
                if t + 1 < nt1:
                    x_ev[0] = S.dma("sp", lambda e: e.dma_start(out=xh[:], in_=fm(xT, 0, KD, t0 + T)), xh_dma, writes=xh_b)
                bias_some(1)
                stage = None
                for wi in range(20):
                    wt, wtb = wget(("win", wi), "win", 0, KD, wi * 256, 256)
                    for jj in range(2):
                        ch = 2 * wi + jj
                        fam, i = ch // 8, ch % 8
                        bk, bkb = bank()
                        for k in range(KD):
                            S.op("pe", mm(bk[:], wt[:, k, jj * P:(jj + 1) * P], xn[:, k, :], k == 0, k == KD - 1), reads=[wtb, xn_b[k]], writes=[bkb])
                        if fam == 0:
                            S.op("act", lambda e, bk=bk, i=i: e.activation(out=a_st[:, i, :], in_=bk[:], func=AF.Copy), reads=[bkb], writes=[ast_b[i]])
                            continue
                        if i % 4 == 0:
                            stage = Stage()
                        dst, dstb = stage.chunk()
                        if fam == 1:
                            S.op("dve", lambda e, bk=bk, i=i, dst=dst: e.tensor_tensor(out=dst, in0=bk[:], in1=a_st[:, i, :], op=ALU.mult), reads=[bkb, ast_b[i]], writes=[dstb])
                        elif fam == 2:
                            S.op("act", lambda e, bk=bk, dst=dst: e.activation(out=dst, in_=bk[:], func=AF.Copy), reads=[bkb], writes=[dstb])
                        else:
                            sq, sqb = tmp()
                            S.op("act", lambda e, sq=sq, bk=bk: e.activation(out=sq, in_=bk[:], func=AF.Square), reads=[bkb], writes=[sqb])
                            b2, b2b = bank()
                            S.op("pe", mm(b2[:], ones[:], sq, True, True), reads=[sqb, ones_b], writes=[b2b])
                            r, rb = tmp()
                            S.op("act", lambda e, r=r, b2=b2: e.activation(out=r, in_=b2[:], func=AF.Sqrt, bias=cst[:, 0:1], scale=1.0 / 128), reads=[b2b, cst_b], writes=[rb])
                            S.op("dve", lambda e, r=r: e.reciprocal(out=r, in_=r), reads=[rb], writes=[rb])
                            gcol = 2 if fam == 3 else 3
                            S.op("dve", lambda e, bk=bk, r=r, dst=dst, gcol=gcol: e.scalar_tensor_tensor(out=dst, in0=bk[:], scalar=cst[:, gcol:gcol + 1], in1=r, op0=ALU.mult, op1=ALU.mult),
                                 reads=[bkb, rb, cst_b], writes=[dstb])
                        if i % 4 == 3:
                            half = i // 4
                            if fam == 4:
                                stage.flush(KTl[t, half * 512:(half + 1) * 512, :].rearrange("(c p) k -> p c k", p=P), KT_b[t][half])
                            else:
                                X, Xb = {1: (Zs, Zs_b), 2: (Bs, Bs_b), 3: (QTs, QT_b)}[fam]
                                stage.flush(fm(X, half * 4, 4, t0), Xb[t][half])
                if t + 1 < nt1:
                    rms_stats()
                for hv in range(NH):
                    wt, wtb = wget(("win", 20 + hv), "win", 0, KD, 5120 + hv * 256, 256)
                    for tb in range(4):
                        bk, bkb = bank()
                        for k in range(KD):
                            S.op("pe", mm(bk[:, 0:256], xn[:, k, tb * P:(tb + 1) * P], wt[:, k, :], k == 0, k == KD - 1), reads=[wtb, xn_b[k]], writes=[bkb])
                        vb = vst_b[tb * 4 + hv]
                        if (tb + hv) % 2 == 0:
                            S.op("act", lambda e, bk=bk, tb=tb, hv=hv: e.activation(out=vst[:, tb, hv * 256:(hv + 1) * 256], in_=bk[:, 0:256], func=AF.Copy), reads=[bkb], writes=[vb])
                        else:
                            S.op("dve", lambda e, bk=bk, tb=tb, hv=hv: e.tensor_copy(out=vst[:, tb, hv * 256:(hv + 1) * 256], in_=bk[:, 0:256]), reads=[bkb], writes=[vb])
                S.dma("sp", lambda e: e.dma_start(out=Vl[t].rearrange("(b p) e -> p b e", p=P), in_=vst[:, :, :]), vst_dma, reads=vst_b, writes=[Vl_b[t]])
                if 'exch' not in _SKIP:
                    grp = [[0, 1, 2, 3], [4, 5, 6, 7]]
                    S.coll(lambda e: e.collective_compute("AllGather", ALU.bypass, replica_groups=grp, ins=[KTl[t]], outs=[KTall[t].rearrange("r n k -> (r n) k")]), coll_sem,
                           reads=KT_b[t], writes=[KTall_b[t]])
                    S.coll(lambda e: e.collective_compute("AllGather", ALU.bypass, replica_groups=grp, ins=[Vl[t]], outs=[Vall[t].rearrange("r n e -> (r n) e")]), coll_sem,
                           reads=[Vl_b[t]], writes=[Vall_b[t]])
                    if t == ROT_SPLIT - 1:
                        S.op("pool", pool_setup)
                        rot_copies("pool", 0, "pk")
                bias_some(1 if t < NT - 1 else 100)
                for gi in range(16):
                    wt, wtb = wget(("wg", gi), "wg", 0, KD, gi * 256, 256)
                    for jj in range(2):
                        gc = 2 * gi + jj
                        bk, bkb = bank()
                        for k in range(KD):
                            S.op("pe", mm(bk[:], wt[:, k, jj * P:(jj + 1) * P], xn[:, k, :], k == 0, k == KD - 1), reads=[wtb, xn_b[k]], writes=[bkb])
                        if gc % 4 == 0:
                            stage = Stage()
                        dst, dstb = stage.chunk()
                        S.op("act", lambda e, bk=bk, dst=dst: e.activation(out=dst, in_=bk[:], func=AF.Sigmoid), reads=[bkb], writes=[dstb])
                        if gc % 4 == 3:
                            q4 = (gc % 16) // 4
                            if gc < 16:
                                stage.flush(fm(GAs, q4 * 4, 4, t0), GA_b[t][q4])
                            else:
                                stage.flush(fm(GBs, q4 * 4, 4, t0), GB_b[t][q4])

            for t in range(nt1):
                phase1_tile(t)

            if 'exch' not in _SKIP:
                zrd = [Zs_b[0][0], Zs_b[0][1], Zs_b[NT - 1][0], Zs_b[NT - 1][1]]
                if 'zedge' not in _SKIP:
                    S.dma("pool", lambda e: e.dma_start(out=zedge[0:1, :].rearrange("o (n u) -> (o n) u", u=1), in_=Zs[:, 0:1], allow_slow_non_contiguous=True), conv_sem, reads=zrd, writes=[conv_chain, zedge_b])
                    S.dma("pool", lambda e: e.dma_start(out=zedge[1:2, :].rearrange("o (n u) -> (o n) u", u=1), in_=Zs[:, TOK - 1:TOK], allow_slow_non_contiguous=True), conv_sem, reads=zrd, writes=[conv_chain, zedge_b])
                grp = [[0, 1, 2, 3], [4, 5, 6, 7]]
                if 'zcoll' not in _SKIP:
                    S.coll(lambda e: e.collective_compute("AllGather", ALU.bypass, replica_groups=grp, ins=[zedge[:, :]], outs=[zedge_all[:, :]]), coll_sem,
                           reads=[zedge_b], writes=[zedge_all_b])


            late_conv(1000, x_ev[0])
            if debug:
                dbg_evs.append(S.dma("pool", lambda e: e.dma_start(out=KTd[:, :, :], in_=KTl[:, :, :]), conv_sem, reads=[b for tb_ in KT_b for b in tb_], writes=[conv_chain]))
                dbg_evs.append(S.dma("pool", lambda e: e.dma_start(out=Vd[:, :, :], in_=Vl[:, :, :]), conv_sem, reads=Vl_b, writes=[conv_chain]))
            def bias_kind(qt, j, h):
                if j < 32:
                    delta = 128 * j - 512 * qt
                    if -128 <= delta <= 512:
                        return ("tile", ((delta + 128) // 128) * 4 + h)
                    return ("far", (8 + h) if delta < 0 else (12 + h))
                if qt == NT - 1 and j == 32:
                    return ("tile", 24 + h)
                if qt == 0 and j == 127:
                    return ("tile", 28 + h)
                sp_ = j // 32
                return ("far", 20 + 4 * (sp_ - 1) + h)

            rot_done = []

            def kv_load(h, sp_, kc):
                K, V, Kb, Vb = kv_ring.next()
                t2 = 2 * kc
                if sp_ == 1 and not rot_done:
                    rot_done.append(1)
                    rot_copies("sp", 1, "rk", (0,))
                if sp_ == 0:
                    ksrc = KTl[t2:t2 + 2, h * 256:(h + 1) * 256, :]
                    vsrc = Vl[t2:t2 + 2, :, h * 256:(h + 1) * 256]
                    kdep = KT_b[t2] + KT_b[t2 + 1]
                    vdep = [Vl_b[t2], Vl_b[t2 + 1]]
                else:
                    ksrc = KTrot[sp_ - 1, t2:t2 + 2, h * 256:(h + 1) * 256, :]
                    vsrc = Vrot[sp_ - 1, t2:t2 + 2, :, h * 256:(h + 1) * 256]
                    kdep = [KTrot_b[sp_ - 1][0 if t2 < ROT_SPLIT else 1]]
                    vdep = [Vrot_b[sp_ - 1][0 if t2 < ROT_SPLIT else 1]]
                for c in range(2):
                    Kd = K[:, c, :].rearrange("p (t k) -> p t k", t=2)
                    ks = ksrc[:, c * P:(c + 1) * P, :].rearrange("t p k -> p t k")
                    S.dma("sp", lambda e, Kd=Kd, ks=ks: e.dma_start(out=Kd, in_=ks), Kb, reads=kdep, writes=[Kb], batch=(c == 1))
                for tt in range(2):
                    Vd = V[:, tt * 4:(tt + 1) * 4, :]
                    vs = vsrc[tt].rearrange("(b p) e -> p b e", p=P)
                    S.dma("sp", lambda e, Vd=Vd, vs=vs: e.dma_start(out=Vd, in_=vs), Vb, reads=vdep, writes=[Vb], batch=(tt == 1))
                if sp_ >= 1 and len(rot_done) == sp_ and sp_ < 3:
                    rot_done.append(1)
                    rot_copies("sp", 1, "rk", (sp_,))
                    if sp_ == 2:
                        for i, rn in enumerate(("zl", "zr")):
                            S.dma("sp", lambda e, i=i, rn=rn: e.dma_start(out=zhalo[i:i + 1, :], in_=zedge_all[bass.ds(regs[rn], 1), :]),
                                  zhalo_b, reads=[zedge_all_b], writes=[zhalo_b])
                return K, V, Kb, Vb

            ATT_CHUNKS = [(h, sp_, kc) for h in range(NH) for sp_ in range(4) for kc in range(4)]
            att_pre = {}

            def attention_prefetch(qt):
                t0 = qt * T
                S.fence(G_b + mixer_bufs + setup_bufs, attn_bufs)
                S.dma("sp", lambda e: e.dma_start(out=qsb, in_=fm(QTs, 0, 8, t0)), qsb_b, reads=QT_b[qt], writes=[qsb_b])
                loaded = [kv_load(*ATT_CHUNKS[0]), kv_load(*ATT_CHUNKS[1])]
                att_pre[qt] = loaded

            def attention(qt, fill=()):
                fill = list(fill)
                t0 = qt * T
                if qt not in att_pre:
                    attention_prefetch(qt)
                pend_B = []
                chunks = ATT_CHUNKS
                loaded = att_pre.pop(qt)
                nl = len(loaded)
                for ci, (h, sp_, kc) in enumerate(chunks):
                    while nl < min(len(chunks), ci + 2):
                        loaded.append(kv_load(*chunks[nl]))
                        nl += 1
                    K, V, Kb, Vb = loaded[ci]
                    if sp_ == 0 and kc == 0:
                        pend = None
                        O = [[(bank_t[4 + 2 * c + ec], bank_b[4 + 2 * c + ec]) for ec in range(2)] for c in range(2)]
                    for blk in range(8):
                        j = sp_ * 32 + kc * 8 + blk
                        kind = bias_kind(qt, j, h)
                        bt = None
                        if kind[0] == "tile":
                            bt, btb = btl_ring.next()
                            S.dma("sp", lambda e, bt=bt, n=kind[1]: e.dma_start(out=bt[:], in_=BT[n]), btb, reads=[BT_b[kind[1]]], writes=[btb])
                        Es = []
                        for c in range(2):
                            sbk, sbkb = s_ring.next()
                            S.op("pe", mm(sbk[:], K[:, c, blk * P:(blk + 1) * P], qsb[:, 2 * h + c, :], True, True), reads=[Kb, qsb_b], writes=[sbkb])
                            E, Eb = E_ring.next()
                            if bt is None:
                                S.op("act", lambda e, E=E, sbk=sbk, col=kind[1]: e.activation(out=E, in_=sbk[:], func=AF.Exp, bias=cst[:, col:col + 1], scale=1.0), reads=[sbkb, cst_b], writes=[Eb])
                            else:
                                tl, tlb = tmp()
                                S.op("dve", lambda e, tl=tl, sbk=sbk, bt=bt: e.tensor_tensor(out=tl, in0=sbk[:], in1=bt[:], op=ALU.add), reads=[sbkb, btb], writes=[tlb])
                                S.op("act", lambda e, E=E, tl=tl: e.activation(out=E, in_=tl, func=AF.Exp), reads=[tlb], writes=[Eb])
                            a_i = 2 * c + (j % 2)
                            seng = "pool" if a_i == 3 else "dve"
                            if j < 2:
                                S.op(seng, lambda e, E=E, a_i=a_i: e.tensor_copy(out=acs_ap[a_i], in_=E), reads=[Eb], writes=[acs_b[a_i]])
                            else:
                                S.op(seng, lambda e, E=E, a_i=a_i: e.tensor_tensor(out=acs_ap[a_i], in0=acs_ap[a_i], in1=E, op=ALU.add), reads=[Eb, acs_b[a_i]], writes=[acs_b[a_i]])
                            Es.append((E, Eb))
                        if pend is not None:
                            emit_pv(pend, O)
                        pend = (Es, V, Vb, blk, j)
                        if h == 1 and j >= 30 and (j - 30) % 10 == 0 and fill:
                            fill.pop(0)()
                        if j == 12 and pend_B:
                            attn_epilogue_B1(qt, pend_B[0])
                        if j == 24 and pend_B:
                            attn_epilogue_B(qt, pend_B.pop())
                    if sp_ == 3 and kc == 3:
                        emit_pv(pend, O)
                        pend = None
                        attn_epilogue_A(qt, h, O)
                        if h == NH - 1:
                            while fill:
                                fill.pop(0)()
                            attn_epilogue_B1(qt, h)
                            attn_epilogue_B(qt, h)
                        else:
                            pend_B.append(h)

            def emit_pv(pend, O):
                Es, V, Vb, blk, j = pend
                for c in range(2):
                    E, Eb = Es[c]
                    for ec in range(2):
                        S.op("pe", mm(O[c][ec][0][:], V[:, blk, ec * P:(ec + 1) * P], E, j == 0, j == 127), reads=[Vb, Eb], writes=[O[c][ec][1]])

            def attn_epilogue_A(qt, h, O):
                for ec in range(2):
                    S.op("act", lambda e, ec=ec: e.activation(out=of_t[ec][:], in_=O[0][ec][0][:], func=AF.Copy), reads=[O[0][ec][1]], writes=[of_b[ec]])
                    S.op("dve", lambda e, ec=ec: e.tensor_copy(out=og_t[ec][:], in_=O[1][ec][0][:]), reads=[O[1][ec][1]], writes=[og_b[ec]])
                S.op("dve", lambda e: e.tensor_tensor(out=acc[:], in0=acs_ap[0], in1=acs_ap[1], op=ALU.add), reads=[acs_b[0], acs_b[1]], writes=[acc_b])
                S.op("dve", lambda e: e.tensor_tensor(out=rstd[:], in0=acs_ap[2], in1=acs_ap[3], op=ALU.add), reads=[acs_b[2], acs_b[3]], writes=[rstd_b])

            def attn_epilogue_B1(qt, h):
                rr = []
                for c in range(2):
                    src, srcb = (acc, acc_b) if c == 0 else (rstd, rstd_b)
                    sbk, sbkb = s_ring.next()
                    S.op("pe", mm(sbk[:], ones[:], src[:], True, True), reads=[ones_b, srcb], writes=[sbkb])
                    r, rb = tmp()
                    S.op("dve", lambda e, r=r, sbk=sbk: e.reciprocal(out=r, in_=sbk[:]), reads=[sbkb], writes=[rb])
                    if c == 1:
                        S.op("dve", lambda e, r=r: e.tensor_scalar(out=r, in0=r, scalar1=cst[:, 1:2], scalar2=None, op0=ALU.mult), reads=[rb, cst_b], writes=[rb])
                    rr.append((r, rb))
                for ec in range(2):
                    S.op("dve", lambda e, ec=ec: e.tensor_tensor(out=og_t[ec][:], in0=og_t[ec][:], in1=rr[1][0], op=ALU.mult), reads=[og_b[ec], rr[1][1]], writes=[og_b[ec]])
                    S.op("dve", lambda e, ec=ec: e.tensor_tensor(out=of_t[ec][:], in0=of_t[ec][:], in1=rr[0][0], op=ALU.mult), reads=[of_b[ec], rr[0][1]], writes=[of_b[ec]])
                    S.op("dve", lambda e, ec=ec: e.tensor_tensor(out=of_t[ec][:], in0=of_t[ec][:], in1=og_t[ec][:], op=ALU.subtract), reads=[of_b[ec], og_b[ec]], writes=[of_b[ec]])

            def attn_epilogue_B(qt, h):
                for ec in range(2):
                    if ec == 0:
                        S.op("act", lambda e, ec=ec: e.activation(out=acc[:], in_=of_t[ec][:], func=AF.Square), reads=[of_b[ec]], writes=[acc_b])
                    else:
                        sq, sqb = tmp()
                        S.op("act", lambda e, sq=sq, ec=ec: e.activation(out=sq, in_=of_t[ec][:], func=AF.Square), reads=[of_b[ec]], writes=[sqb])
                        S.op("dve", lambda e, sq=sq: e.tensor_tensor(out=acc[:], in0=acc[:], in1=sq, op=ALU.add), reads=[sqb, acc_b], writes=[acc_b])
                sbk, sbkb = s_ring.next()
                S.op("pe", mm(sbk[:], ones[:], acc[:], True, True), reads=[ones_b, acc_b], writes=[sbkb])
                S.op("act", lambda e, sbk=sbk: e.activation(out=rstd[:], in_=sbk[:], func=AF.Sqrt, bias=cst[:, 0:1], scale=1.0 / 256), reads=[sbkb, cst_b], writes=[rstd_b])
                S.op("dve", lambda e: e.reciprocal(out=rstd[:], in_=rstd[:]), reads=[rstd_b], writes=[rstd_b])
                for ec in range(2):
                    S.op("dve", lambda e, ec=ec: e.scalar_tensor_tensor(out=onT[:, 2 * h + ec, :], in0=of_t[ec][:], scalar=cst[:, 4 + ec:5 + ec], in1=rstd[:], op0=ALU.mult, op1=ALU.mult),
                         reads=[of_b[ec], rstd_b, cst_b], writes=[onT_b[2 * h + ec]])

            def conv_loads(qt):
                t0 = qt * T
                lo = 1 if qt == 0 else 0
                hi = 513 if qt == NT - 1 else 514
                zr = []
                for tt in (qt - 1, qt, qt + 1):
                    if 0 <= tt < NT:
                        zr += Zs_b[tt]
                S.dma("sp", lambda e: e.dma_start(out=zsb[:, :, lo:hi], in_=fm(Zs, 0, 8, t0 - 1 + lo, hi - lo)), zsb_b, reads=zr, writes=[zsb_b])
                if qt == 0 or qt == NT - 1:
                    col = 0 if qt == 0 else 513
                    zi = 0 if qt == 0 else 1
                    vcol = 3 if qt == 0 else 4
                    S.dma("sp", lambda e: e.dma_start(out=zsb[:, :, col:col + 1], in_=zhalo[zi:zi + 1, :].rearrange("o (c p u) -> p (o c) u", p=P, u=1), allow_slow_non_contiguous=True),
                          zsb_b, reads=[zhalo_b], writes=[zsb_b])
                    S.op("dve", lambda e: e.tensor_scalar(out=zsb[:, :, col:col + 1], in0=zsb[:, :, col:col + 1], scalar1=sel[:, vcol:vcol + 1], scalar2=None, op0=ALU.mult), reads=[zsb_b, const_b], writes=[zsb_b])
                S.dma("sp", lambda e: e.dma_start(out=bsb, in_=fm(Bs, 0, 8, t0)), bsb_b, reads=Bs_b[qt], writes=[bsb_b])

            def conv_chunk(qt, ch):
                if True:
                    y, yb = tmp()
                    S.op("dve", lambda e, y=y, ch=ch: e.tensor_scalar(out=y, in0=zsb[:, ch, 0:T], scalar1=small[:, ch:ch + 1], scalar2=None, op0=ALU.mult), reads=[zsb_b, const_b], writes=[yb])
                    S.op("dve", lambda e, y=y, ch=ch: e.scalar_tensor_tensor(out=y, in0=zsb[:, ch, 1:T + 1], scalar=small[:, 8 + ch:9 + ch], in1=y, op0=ALU.mult, op1=ALU.add), reads=[zsb_b, const_b, yb], writes=[yb])
                    S.op("dve", lambda e, y=y, ch=ch: e.scalar_tensor_tensor(out=y, in0=zsb[:, ch, 2:T + 2], scalar=small[:, 16 + ch:17 + ch], in1=y, op0=ALU.mult, op1=ALU.add), reads=[zsb_b, const_b, yb], writes=[yb])
                    S.op("dve", lambda e, y=y, ch=ch: e.tensor_tensor(out=yain[:, ch, :], in0=y, in1=bsb[:, ch, :], op=ALU.mult), reads=[yb, bsb_b], writes=[yain_b[ch]])

            def mixer(qt):
                t0 = qt * T
                S.fence(attn_bufs + G_b, mixer_bufs)
                for mq in range(4):
                    ga, gab = ga_ring.next()
                    gb, gbb = gb_ring.next()
                    S.dma("sp", lambda e, ga=ga, mq=mq: e.dma_start(out=ga, in_=fm(GAs, mq * 4, 4, t0)), gab, reads=[GA_b[qt][mq]], writes=[gab])
                    S.dma("sp", lambda e, gb=gb, mq=mq: e.dma_start(out=gb, in_=fm(GBs, mq * 4, 4, t0)), gbb, reads=[GB_b[qt][mq]], writes=[gbb])
                    wat, wab = wget(("wa", mq), "wa", 0, 8, mq * 512, 512)
                    wbt, wbb = wget(("wb", mq), "wb", 0, 8, mq * 512, 512)
                    for mi in range(4):
                        m = mq * 4 + mi
                        ba, bab = bank()
                        bb_, bbb = bank()
                        for k in range(8):
                            S.op("pe", mm(ba[:], wat[:, k, mi * P:(mi + 1) * P], yain[:, k, :], k == 0, k == 7), reads=[wab, yain_b[k]], writes=[bab])
                        for k in range(8):
                            S.op("pe", mm(bb_[:], wbt[:, k, mi * P:(mi + 1) * P], onT[:, k, :], k == 0, k == 7), reads=[wbb, onT_b[k]], writes=[bbb])
                        ta, tab_ = tmp()
                        tb_, tbb = tmp()
                        S.op("dve", lambda e, ta=ta, ba=ba, ga=ga, mi=mi: e.tensor_tensor(out=ta, in0=ba[:], in1=ga[:, mi, :], op=ALU.mult), reads=[bab, gab], writes=[tab_])
                        S.op("dve", lambda e, tb_=tb_, bb_=bb_, gb=gb, mi=mi: e.tensor_tensor(out=tb_, in0=bb_[:], in1=gb[:, mi, :], op=ALU.mult), reads=[bbb, gbb], writes=[tbb])
                        S.op("dve", lambda e, ta=ta, tb_=tb_, m=m: e.tensor_tensor(out=mix[:, m, :], in0=ta, in1=tb_, op=ALU.add), reads=[tab_, tbb], writes=[mix_b[m]])
                if debug:
                    dbg_evs.append(S.dma("sp", lambda e: e.dma_start(out=fm(ONd, 0, 8, t0), in_=onT), vst_dma, reads=onT_b))
                    dbg_evs.append(S.dma("sp", lambda e: e.dma_start(out=fm(MIXd, 0, KD, t0), in_=mix), vst_dma, reads=mix_b))
                for wi in range(8):
                    wt, wtb = wget(("wo", wi), "wo", 0, KD, wi * 256, 256)
                    for jj in range(2):
                        m = 2 * wi + jj
                        bk, bkb = bank()
                        for k in range(KD):
                            S.op("pe", mm(bk[:], wt[:, k, jj * P:(jj + 1) * P], mix[:, k, :], k == 0, k == KD - 1), reads=[wtb, mix_b[k]], writes=[bkb])
                        S.op("dve", lambda e, bk=bk, m=m: e.tensor_tensor(out=xh[:, m, :], in0=bk[:], in1=xh[:, m, :], op=ALU.add), reads=[bkb, xh_b[m]], writes=[xh_b[m]])
                if debug:
                    dbg_evs.append(S.dma("sp", lambda e: e.dma_start(out=fm(H2d, 0, KD, t0), in_=xh[:]), xh_dma, reads=xh_b))

            def ple(qt):
                t0 = qt * T
                S.dma("pool", lambda e: e.dma_start(out=pTb[:], in_=fm(pT, 0, 2, t0)), pTb_b, writes=[pTb_b])
                for wi in range(8):
                    wt, wtb = wget(("wpg", wi), "wpg", 0, KD, wi * 256, 256)
                    wpt, wpb = wget(("wpp", wi), "wpp", 0, 2, wi * 256, 256)
                    for jj in range(2):
                        m = 2 * wi + jj
                        bg, bgb = bank()
                        bp, bpb = bank()
                        for k in range(KD):
                            S.op("pe", mm(bg[:], wt[:, k, jj * P:(jj + 1) * P], xn[:, k, :], k == 0, k == KD - 1), reads=[wtb, xn_b[k]], writes=[bgb])
                        for k in range(2):
                            S.op("pe", mm(bp[:], wpt[:, k, jj * P:(jj + 1) * P], pTb[:, k, :], k == 0, k == 1), reads=[wpb, pTb_b], writes=[bpb])
                        sg, sgb = tmp()
                        S.op("act", lambda e, sg=sg, bg=bg: e.activation(out=sg, in_=bg[:], func=AF.Sigmoid), reads=[bgb], writes=[sgb])
                        S.op("dve", lambda e, sg=sg, bp=bp: e.tensor_tensor(out=sg, in0=sg, in1=bp[:], op=ALU.mult), reads=[sgb, bpb], writes=[sgb])
                        S.op("dve", lambda e, sg=sg, m=m: e.tensor_tensor(out=xh[:, m, :], in0=sg, in1=xh[:, m, :], op=ALU.add), reads=[sgb, xh_b[m]], writes=[xh_b[m]])
                out_evs.append(S.dma("sp", lambda e: e.dma_start(out=fm(outT, 0, KD, t0), in_=xh[:]), xh_dma, reads=xh_b))

            S.fence(ast_b, [zsb_b, bsb_b] + yain_b)
            S.fence(vst_b, onT_b)
            for qt in range(nt2):
                if qt == 0:
                    attention(qt)
                    S.dma("sp", lambda e, qt=qt: e.dma_start(out=xh[:], in_=fm(Hs, 0, KD, qt * T)), xh_dma, reads=[Hs_b[qt]], writes=xh_b)
                    conv_loads(qt)
                    for ch in range(8):
                        conv_chunk(qt, ch)
                else:
                    conv_loads(qt)
                    attention(qt, [(lambda qt=qt, ch=ch: conv_chunk(qt, ch)) for ch in range(8)])
                    S.dma("sp", lambda e, qt=qt: e.dma_start(out=xh[:], in_=fm(Hs, 0, KD, qt * T)), xh_dma, reads=[Hs_b[qt]], writes=xh_b)
                mixer(qt)
                rmsnorm(32)
                S.fence(attn_bufs + mixer_bufs, G_b)
                ffn("b")
                if qt + 1 < nt2:
                    attention_prefetch(qt + 1)
                rmsnorm(48)
                ple(qt)

            S.op("sp", None, extra=out_evs + dbg_evs)
            return ws.plan

        plan = emit(NullSched(), None)
        S = Sched(nc)
        emit(S, plan)
        stats = S.emit(st)
        print("sched stats (n_instr, n_waits):", stats, "sems:", len(S.sems) + 5, flush=True)
    return nc


def _prep_shared(inputs):
    f = lambda a: np.ascontiguousarray(np.asarray(a, dtype=np.float32))
    sh = {}
    for k in ("ffn1_w1", "ffn1_w3", "ffn1_w2", "w_in", "w_gate", "w_branch_a", "w_branch_b", "w_out",
              "ffn2_w1", "ffn2_w3", "ffn2_w2", "w_ple_gate", "w_ple_proj"):
        sh[k] = f(inputs[k][0])
    g = np.zeros((P, 64), np.float32)
    for i, k in enumerate(("ffn1_norm", "mix_norm", "ffn2_norm", "ple_norm")):
        g[:, 16 * i:16 * (i + 1)] = f(inputs[k][0]).reshape(KD, P).T
    sh["gains"] = g
    sm = np.zeros((P, 40), np.float32)
    cw = f(inputs["conv_w"][0])
    for tap in range(3):
        sm[:, tap * 8:(tap + 1) * 8] = cw[tap].reshape(8, P).T
    sm[:, 24] = f(inputs["q_norm"][0])
    sm[:, 25] = f(inputs["k_norm"][0])
    sm[:, 26] = f(inputs["lam_q1"][0])
    sm[:, 27] = f(inputs["lam_k1"][0])
    sm[:, 28] = f(inputs["lam_q2"][0])
    sm[:, 29] = f(inputs["lam_k2"][0])
    sm[:, 30:32] = f(inputs["sub_norm"][0]).reshape(2, P).T
    sh["small"] = sm
    rb = f(inputs["rel_bias"])
    sh["tab32"] = rb
    sh["relb"] = np.ascontiguousarray(np.broadcast_to(rb.reshape(1, 128), (P, 128)))
    sh["onehot"] = _onehot_const()
    return sh


def _prep_core(inputs, c):
    b, r = c // 4, c % 4
    s0 = r * TOK
    m = {}
    m["xT"] = np.ascontiguousarray(np.asarray(inputs["x"][b, s0:s0 + TOK, :], dtype=np.float32).T)
    m["pT"] = np.ascontiguousarray(np.asarray(inputs["p"][0, b, s0:s0 + TOK, :], dtype=np.float32).T)
    sel = np.zeros((P, 8), np.float32)
    for sp_ in range(1, 4):
        sel[:, sp_ - 1] = 1.0 if r + sp_ < 4 else 0.0
    sel[:, 3] = 1.0 if r > 0 else 0.0
    sel[:, 4] = 1.0 if r < 3 else 0.0
    m["sel"] = sel
    offs = np.zeros((1, 16), np.int32)
    for sp_ in range(1, 4):
        offs[0, sp_ - 1] = (r + sp_) % 4
    offs[0, 3] = 2 * ((r - 1) % 4) + 1
    offs[0, 4] = 2 * ((r + 1) % 4)
    m["offs"] = offs
    return m


_NC_CACHE = {}
_SKIP = set()
_DBGSET = set()


def kernel(**inputs):
    if "nc" not in _NC_CACHE:
        _NC_CACHE["nc"] = build_program(False)
    nc = _NC_CACHE["nc"]
    sh = _prep_shared(inputs)
    in_maps = []
    for c in range(8):
        m = dict(sh)
        m.update(_prep_core(inputs, c))
        in_maps.append(m)
    res = run_bass_kernel_spmd(nc, in_maps, core_ids=list(range(8)))
    out = np.empty((2, 4 * TOK, D), np.float32)
    for c in range(8):
        b, r = c // 4, c % 4
        out[b, r * TOK:(r + 1) * TOK, :] = np.asarray(res.results[c]["outT"], dtype=np.float32).T
    return out
```
